# Optimizing a Trainium2 kernel written in Bass

```python
import jax, jax.numpy as jnp
from jax import lax
import numpy as np

D_MODEL = 1024
BATCH = 32
SEQ = 2048
DEPTH = 1

CHUNK = 64
LRU_WIDTH = 1280
LRU_HEADS = 10
LRU_HEAD_DIM = LRU_WIDTH // LRU_HEADS
CONV_WIDTH = 4
LRU_C = 8.0
SGU_WIDTH = 768
SGU_GROUPS = 6
SGU_GROUP_DIM = SGU_WIDTH // SGU_GROUPS
SGU_BLOCK = 128
D_FF = 4 * D_MODEL
N_BRANCH = 2
D_IN = 2 * LRU_WIDTH + 2 * SGU_WIDTH + N_BRANCH * D_MODEL
IN_SPLITS = (LRU_WIDTH, 2 * LRU_WIDTH, 2 * LRU_WIDTH + SGU_WIDTH,
             2 * LRU_WIDTH + 2 * SGU_WIDTH, 2 * LRU_WIDTH + 2 * SGU_WIDTH + D_MODEL)
ALPHA = (2.0 * DEPTH) ** 0.25
BETA = (8.0 * DEPTH) ** -0.25
LN_EPS = 1e-5

kernel_name = "hawk_gmlp_hybrid_deepnorm_adaln"


def _layer_norm(x, g, b):
    xf = x.astype(jnp.float32)
    mu = jnp.mean(xf, axis=-1, keepdims=True)
    var = jnp.mean(jnp.square(xf - mu), axis=-1, keepdims=True)
    y = (xf - mu) * lax.rsqrt(var + LN_EPS)
    return (y * g.astype(jnp.float32) + b.astype(jnp.float32)).astype(x.dtype)


def _causal_depthwise_conv(x, w, b):
    y = lax.conv_general_dilated(
        x, w[:, None, :].astype(x.dtype), window_strides=(1,),
        padding=[(CONV_WIDTH - 1, 0)], dimension_numbers=("NWC", "WIO", "NWC"),
        feature_group_count=x.shape[-1])
    return y + b


def _rg_lru(x, w_a, b_a, w_x, b_x, lam):
    B, S, _ = x.shape
    xh = x.reshape(B, S, LRU_HEADS, LRU_HEAD_DIM)
    r = jax.nn.sigmoid(jnp.einsum("bshi,hij->bshj", xh, w_a).reshape(B, S, LRU_WIDTH) + b_a)
    i = jax.nn.sigmoid(jnp.einsum("bshi,hij->bshj", xh, w_x).reshape(B, S, LRU_WIDTH) + b_x)
    log_a = (-LRU_C * jax.nn.softplus(-lam.astype(jnp.float32))) * r.astype(jnp.float32)
    a = jnp.exp(log_a)
    inp = jnp.sqrt(-jnp.expm1(2.0 * log_a)) * (i * x).astype(jnp.float32)

    def step(h, ab):
        a_t, b_t = ab
        h = a_t * h + b_t
        return h, h

    h0 = jnp.zeros((B, LRU_WIDTH), jnp.float32)
    _, hs = lax.scan(step, h0, (jnp.swapaxes(a, 0, 1), jnp.swapaxes(inp, 0, 1)))
    return jnp.swapaxes(hs, 0, 1).astype(x.dtype)


def _spatial_gating(u, v, w_sp, b_sp, ln_g, ln_b):
    B, S, _ = u.shape
    v = _layer_norm(v, ln_g, ln_b)
    nblk = S // SGU_BLOCK
    vb = v.reshape(B, nblk, SGU_BLOCK, SGU_GROUPS, SGU_GROUP_DIM)
    pos = jnp.arange(SGU_BLOCK)
    mask = (pos[None, :] // CHUNK) <= (pos[:, None] // CHUNK)
    w = jnp.where(mask[None], w_sp, 0.0).astype(v.dtype)
    mixed = jnp.einsum("gts,bnsgc->bntgc", w, vb) + jnp.transpose(b_sp)[None, None, :, :, None]
    return u * mixed.reshape(B, S, SGU_WIDTH)


def setup_inputs(seed: int = 0) -> dict:
    key = jax.random.key(seed)
    ks = jax.random.split(key, 28)
    L = DEPTH
    nrm = lambda k, shape, s: jax.random.normal(k, shape, jnp.float32) * s
    u = jax.random.uniform(ks[12], (L, LRU_WIDTH), jnp.float32, 0.9, 0.999)
    a0 = u ** (1.0 / LRU_C)
    lru_lambda = jnp.log(a0) - jnp.log1p(-a0)
    return {
        "x": nrm(ks[0], (BATCH, SEQ, D_MODEL), 1.0),
        "c": nrm(ks[1], (BATCH, D_MODEL), 1.0),
        "w_ada": nrm(ks[2], (L, D_MODEL, 6 * D_MODEL), 0.1 * D_MODEL ** -0.5),
        "b_ada": nrm(ks[3], (L, 6 * D_MODEL), 0.01),
        "w_in": nrm(ks[4], (L, D_MODEL, D_IN), D_MODEL ** -0.5),
        "b_in": nrm(ks[5], (L, D_IN), 0.01),
        "w_conv": nrm(ks[6], (L, CONV_WIDTH, LRU_WIDTH), CONV_WIDTH ** -0.5),
        "b_conv": nrm(ks[7], (L, LRU_WIDTH), 0.01),
        "w_rg_a": nrm(ks[8], (L, LRU_HEADS, LRU_HEAD_DIM, LRU_HEAD_DIM), LRU_HEAD_DIM ** -0.5),
        "b_rg_a": nrm(ks[9], (L, LRU_WIDTH), 0.01),
        "w_rg_x": nrm(ks[10], (L, LRU_HEADS, LRU_HEAD_DIM, LRU_HEAD_DIM), LRU_HEAD_DIM ** -0.5),
        "b_rg_x": nrm(ks[11], (L, LRU_WIDTH), 0.01),
        "lru_lambda": lru_lambda,
        "w_sp": nrm(ks[13], (L, SGU_GROUPS, SGU_BLOCK, SGU_BLOCK), SGU_BLOCK ** -0.5),
        "b_sp": 1.0 + nrm(ks[14], (L, SGU_GROUPS, SGU_BLOCK), 0.01),
        "ln_v_g": 1.0 + nrm(ks[15], (L, SGU_WIDTH), 0.01),
        "ln_v_b": nrm(ks[16], (L, SGU_WIDTH), 0.01),
        "w_o_lru": nrm(ks[17], (L, LRU_WIDTH, D_MODEL), BETA * LRU_WIDTH ** -0.5),
        "w_o_sgu": nrm(ks[18], (L, SGU_WIDTH, D_MODEL), BETA * SGU_WIDTH ** -0.5),
        "w_out": nrm(ks[19], (L, D_MODEL, D_MODEL), BETA * D_MODEL ** -0.5),
        "ln1_g": 1.0 + nrm(ks[20], (L, D_MODEL), 0.01),
        "ln1_b": nrm(ks[21], (L, D_MODEL), 0.01),
        "w_up": nrm(ks[22], (L, D_MODEL, D_FF), BETA * D_MODEL ** -0.5),
        "w_down": nrm(ks[23], (L, D_FF, D_MODEL), BETA * D_FF ** -0.5),
        "ln2_g": 1.0 + nrm(ks[24], (L, D_MODEL), 0.01),
        "ln2_b": nrm(ks[25], (L, D_MODEL), 0.01),
    }


def reference(x, c, w_ada, b_ada, w_in, b_in, w_conv, b_conv, w_rg_a, b_rg_a, w_rg_x, b_rg_x,
              lru_lambda, w_sp, b_sp, ln_v_g, ln_v_b, w_o_lru, w_o_sgu, w_out, ln1_g, ln1_b,
              w_up, w_down, ln2_g, ln2_b):
    c_act = jax.nn.silu(c)
    for l in range(DEPTH):
        mod = c_act @ w_ada[l] + b_ada[l]
        sh1, sc1, gt1, sh2, sc2, gt2 = jnp.split(mod, 6, axis=-1)

        h = x * (1.0 + sc1[:, None, :]) + sh1[:, None, :]
        proj = h @ w_in[l] + b_in[l]
        x_lru, g_lru, u, v, gate_a, gate_b = jnp.split(proj, IN_SPLITS, axis=-1)

        xc = _causal_depthwise_conv(x_lru, w_conv[l], b_conv[l])
        y_lru = _rg_lru(xc, w_rg_a[l], b_rg_a[l], w_rg_x[l], b_rg_x[l], lru_lambda[l])
        y_a = (y_lru * jax.nn.gelu(g_lru)) @ w_o_lru[l]

        y_sgu = _spatial_gating(jax.nn.gelu(u), jax.nn.gelu(v), w_sp[l], b_sp[l], ln_v_g[l], ln_v_b[l])
        y_b = y_sgu @ w_o_sgu[l]

        merged = jax.nn.sigmoid(gate_a) * y_a + jax.nn.sigmoid(gate_b) * y_b
        mix = merged @ w_out[l]
        x = _layer_norm(ALPHA * x + (1.0 + gt1[:, None, :]) * mix, ln1_g[l], ln1_b[l])

        h2 = x * (1.0 + sc2[:, None, :]) + sh2[:, None, :]
        f = jnp.square(jax.nn.relu(h2 @ w_up[l])) @ w_down[l]
        x = _layer_norm(ALPHA * x + (1.0 + gt2[:, None, :]) * f, ln2_g[l], ln2_b[l])
    return x
```

```python
import contextlib
import numpy as np
import concourse.bass as bass
import concourse.mybir as mybir
from concourse.bass_utils import run_bass_kernel_spmd

F32 = mybir.dt.float32
BF16 = mybir.dt.bfloat16
AF = mybir.ActivationFunctionType
ALU = mybir.AluOpType

NCORES = 8
D = 1024
S = 2048
BL = 4
T = 512
NCH = S // T
LW = 1280
LC = LW // 128
SW = 768
SC = SW // 128
DFF = 4096
FC = DFF // 128
KC = D // 128
ALPHA = 2.0 ** 0.25
LN_EPS = 1e-5
SLOTW = 4096
NSLOT = 3

COMPUTE = ("pe", "act", "dve", "pool")
QUEUES = ("pe", "act", "dve", "pool", "sp")


class Buf:
    __slots__ = ("name", "last_w", "readers")

    def __init__(self, name):
        self.name = name
        self.last_w = None
        self.readers = []


class Op:
    __slots__ = ("q", "fn", "deps", "milestone", "count", "dma_key", "dma_n")

    def __init__(self, q, fn):
        self.q = q
        self.fn = fn
        self.deps = []
        self.milestone = False
        self.count = None
        self.dma_key = None
        self.dma_n = 0


class Sched:
    def __init__(self):
        self.ops = {q: [] for q in QUEUES}
        self.dma_count = {}

    def _add_dep(self, op, prod, kind):
        if prod is None or prod is op:
            return
        if prod.dma_key is None and op.dma_key is None and prod.q == op.q:
            if op.q == "pe" or kind != "raw":
                return
        op.deps.append(prod)
        if prod.dma_key is None:
            prod.milestone = True

    def _mk(self, q, fn, reads, writes, dma_key=None):
        o = Op(q, fn)
        o.dma_key = dma_key
        for b in reads:
            self._add_dep(o, b.last_w, "raw")
        for b in writes:
            self._add_dep(o, b.last_w, "waw")
            for r in b.readers:
                self._add_dep(o, r, "war")
        for b in reads:
            b.readers.append(o)
        for b in writes:
            b.last_w = o
            b.readers = []
        self.ops[q].append(o)
        return o

    def op(self, q, fn, reads=(), writes=()):
        return self._mk(q, fn, reads, writes)

    def dma(self, q, fn, key, r=(), w=(), n=1):
        reads, writes = r, w
        o = self._mk(q, fn, reads, writes, dma_key=key)
        self.dma_count[key] = self.dma_count.get(key, 0) + n
        o.dma_n = self.dma_count[key]
        return o

    def emit(self, nc):
        with contextlib.ExitStack() as es:
            sems = {}
            for q in COMPUTE:
                sems[q] = es.enter_context(nc.semaphore("s_" + q))
            for k in self.dma_count:
                sems[("dma", k)] = es.enter_context(nc.semaphore("d_" + str(k)))
            for q in QUEUES:
                c = 0
                for o in self.ops[q]:
                    if o.dma_key is None and o.milestone:
                        c += 1
                        o.count = c
            block = es.enter_context(nc.Block())
            engs = {"pe": block.tensor, "act": block.scalar, "dve": block.vector,
                    "pool": block.gpsimd, "sp": block.sync}

            def make(q):
                def body(eng):
                    seen = {}
                    for o in self.ops[q]:
                        need = {}
                        for p in o.deps:
                            if p.dma_key is not None:
                                k = ("dma", p.dma_key)
                                v = 16 * p.dma_n
                            else:
                                k = p.q
                                v = p.count
                            if v > need.get(k, 0):
                                need[k] = v
                        for k, v in need.items():
                            if v > seen.get(k, 0):
                                eng.wait_ge(sems[k], v)
                                seen[k] = v
                        if o.dma_key is not None:
                            o.fn(eng, sems[("dma", o.dma_key)])
                        else:
                            ins = o.fn(eng)
                            if o.milestone:
                                ins.then_inc(sems[q], 1)
                return body

            for q in QUEUES:
                if self.ops[q]:
                    engs[q](make(q))


def piece_table():
    P = []
    for j in range(12):
        P.append(("w_in", j, 8, 512))
    P.append(("rg", 0, 20, 128))
    for j in range(4):
        P.append(("w_o_lru", j, 10, 256))
    for j in range(2):
        P.append(("w_o_sgu", j, 6, 512))
    for j in range(2):
        P.append(("w_out", j, 8, 512))
    for j in range(8):
        P.append(("w_up", j, 8, 512))
    for j in range(8):
        P.append(("w_down", j, 32, 128))
    return P


PIECES = piece_table()
PIDX = {(n, j): i for i, (n, j, _, _) in enumerate(PIECES)}


def build_nc():
    nc = bass.Bass("TRN2", target_bir_lowering=False)
    S_ = Sched()

    def din(name, shape, dt=F32):
        return nc.dram_tensor(name, list(shape), dt, kind="ExternalInput").ap()

    x = din("x", [BL * S, D])
    out = nc.dram_tensor("out", [BL * S, D], F32, kind="ExternalOutput").ap()
    c_t = din("c_t", [128, KC, BL])
    w_ada = din("w_ada", [1, D, 6 * D])
    b_ada_c = din("b_ada_c", [128, 48])
    w_in = din("w_in", [1, D, 6144])
    b_in_c = din("b_in_c", [128, 48])
    b_in_v = din("b_in_v", [1, SW])
    convw_c = din("convw_c", [128, LC, 4])
    convb_c = din("convb_c", [128, LC])
    w_rg_a = din("w_rg_a", [1, LC, 128, 128])
    w_rg_x = din("w_rg_x", [1, LC, 128, 128])
    b_rga_c = din("b_rga_c", [128, LC])
    b_rgx_c = din("b_rgx_c", [128, LC])
    lam_c = din("lam_c", [128, LC])
    w_sp = din("w_sp", [1, SC, 128, 128])
    bsp_row = din("bsp_row", [1, SW])
    lnvg_c = din("lnvg_c", [128, SC])
    lnvb_row = din("lnvb_row", [1, SW])
    w_o_lru = din("w_o_lru", [1, LW, D])
    w_o_sgu = din("w_o_sgu", [1, SW, D])
    w_out = din("w_out", [1, D, D])
    w_up = din("w_up", [1, D, DFF])
    w_down = din("w_down", [1, DFF, D])
    ln1g_c = din("ln1g_c", [128, KC])
    ln1b_c = din("ln1b_c", [128, KC])
    g2b = din("g2b", [128, D])
    b2b = din("b2b", [128, D])
    ident_in = din("ident", [128, 128])
    wsc = nc.dram_tensor("wsc", [len(PIECES), 128, SLOTW], BF16, kind="Internal").ap()

    srcs = {"w_in": w_in, "w_o_lru": w_o_lru, "w_o_sgu": w_o_sgu, "w_out": w_out,
            "w_up": w_up, "w_down": w_down}

    with contextlib.ExitStack() as es:
        def TS(name, shape, dt):
            return es.enter_context(nc.sbuf_tensor(name, list(shape), dt))

        def PS(name, shape, dt):
            return es.enter_context(nc.psum_tensor(name, list(shape), dt))

        XNs = [TS("XNa", [128, 4, D], F32), TS("XNb", [128, 4, D], F32)]
        bXNs = [[Buf("XNa%d" % i) for i in range(4)], [Buf("XNb%d" % i) for i in range(4)]]
        H2 = TS("H2", [128, KC, T], BF16)
        bH2 = [Buf("H2%d" % i) for i in range(KC)]
        hT = TS("hT", [128, KC, T], BF16)
        bhT = [Buf("hT%d" % i) for i in range(KC)]
        BIG = TS("BIG", [128, FC, T], BF16)
        bBIG = [Buf("BIG%d" % i) for i in range(FC)]
        XL = TS("XL", [128, LC, T + 4], BF16)
        bXL = [Buf("XL%d" % i) for i in range(LC)]
        AR = TS("AR", [128, LC, T], F32)
        bAR = [Buf("AR%d" % i) for i in range(LC)]
        GG = TS("GG", [128, LC, T], BF16)
        bGG = [Buf("GG%d" % i) for i in range(LC)]
        UG = TS("UG", [128, SC, T], BF16)
        bUG = [Buf("UG%d" % i) for i in range(SC)]
        VN = TS("VN", [128, 4, SW], BF16)
        bVN = [Buf("VN%d" % i) for i in range(4)]
        SIG = TS("SIG", [128, 16, T], BF16)
        bSIG = [Buf("SIG%d" % i) for i in range(16)]
        MG = TS("MG", [128, KC, T], BF16)
        bMG = [Buf("MG%d" % i) for i in range(KC)]
        slots = [TS("slot%d" % i, [128, SLOTW], BF16) for i in range(NSLOT)]
        bslot = [Buf("slot%d" % i) for i in range(NSLOT)]
        bscr = [Buf("scr%d" % i) for i in range(len(PIECES))]

        class Ring:
            def __init__(self, name, n, shape, dt):
                self.t = [TS("%s%d" % (name, i), shape, dt) for i in range(n)]
                self.b = [Buf("%s%d" % (name, i)) for i in range(n)]
                self.i = 0

            def next(self):
                k = self.i % len(self.t)
                self.i += 1
                self.k = k
                return self.t[k], self.b[k]

        rF = Ring("rF", 3, [128, T], F32)
        rV = Ring("rV", 1, [128, SW], F32)

        IDN = TS("IDN", [128, 128], F32)
        bIDN = Buf("IDN")
        G2B = TS("G2B", [128, D], F32)
        B2B = TS("B2B", [128, D], F32)
        bG2B = Buf("G2B")
        BINC = TS("BINC", [128, 48], F32)
        BADA = TS("BADA", [128, 48], F32)
        CVW = TS("CVW", [128, LC, 4], F32)
        CVB = TS("CVB", [128, LC], F32)
        BRA = TS("BRA", [128, LC], F32)
        BRX = TS("BRX", [128, LC], F32)
        CC = TS("CC", [128, LC], F32)
        CCH = TS("CCH", [128, LC], F32)
        BRAH = TS("BRAH", [128, LC], F32)
        BRXH = TS("BRXH", [128, LC], F32)
        BINH = TS("BINH", [128, 48], F32)
        LVG = TS("LVG", [128, SC], F32)
        L1G = TS("L1G", [128, KC], F32)
        L1B = TS("L1B", [128, KC], F32)
        AG1 = TS("AG1", [128, KC], F32)
        AB1 = TS("AB1", [128, KC], F32)
        bC = Buf("consts")
        ROWB = TS("ROWB", [1, SW + 128], BF16)
        bROW = Buf("rows")
        ONEC = TS("ONEC", [128, 1], F32)
        MHALF = TS("MHALF", [128, 4], F32)
        CT = TS("CT", [128, KC, BL], F32)
        CSG = TS("CSG", [128, KC, BL], F32)
        CACT = TS("CACT", [128, KC, BL], BF16)
        bCACT = Buf("cact")
        MOD = TS("MOD", [128, 48, BL], F32)
        bMOD = Buf("MOD")
        S1P = TS("S1P", [128, KC, BL], F32)
        G1P = TS("G1P", [128, KC, BL], F32)
        G2P = TS("G2P", [128, KC, BL], F32)
        A2 = TS("A2", [128, KC, BL], F32)
        B2 = TS("B2", [128, KC, BL], F32)
        bDER = Buf("derived")
        WTB = TS("WTB", [128, SC, 128], BF16)
        BB = TS("BB", [128, SC, 128], F32)
        bWT = Buf("WT")
        HST = TS("HST", [128, LC], F32)
        bHST = [Buf("HST%d" % i) for i in range(LC)]
        ST6 = TS("ST6", [128, 4, 2, 6], F32)
        MV = TS("MV", [128, 4, 2], F32)
        VE = TS("VE", [128, 4], F32)
        RS = TS("RS", [128, 4], F32)
        NMR = TS("NMR", [128, 4], F32)
        bSTAT = Buf("stat")
        ST6V = TS("ST6V", [128, 2, 6], F32)
        MVV = TS("MVV", [128, 2], F32)
        VEV = TS("VEV", [128, 1], F32)
        RSV = TS("RSV", [128, 1], F32)
        NMRV = TS("NMRV", [128, 1], F32)
        bSTATV = Buf("statv")
        JUNK = TS("JUNK", [128, 8], F32)
        bJ = Buf("junk")

        banks = [PS("ps%d" % i, [128, 512], F32) for i in range(8)]
        bbank = [Buf("ps%d" % i) for i in range(8)]
        bi = [0]

        def nb():
            k = bi[0] % 8
            bi[0] += 1
            return banks[k], bbank[k]

        def act(out_, in_, func, bias=None, scale=None, r=(), w=()):
            kw = {}
            if bias is not None:
                kw["bias"] = bias
            if scale is not None:
                kw["scale"] = scale
            S_.op("act", lambda e: e.activation(out=out_, in_=in_, func=func, **kw), r, w)

        def mm(out_, lhsT, rhs, start, stop, r=(), w=()):
            S_.op("pe", lambda e: e.matmul(out_, lhsT=lhsT, rhs=rhs, start=start, stop=stop), r, w)

        def tr(out_, in_, r=(), w=()):
            S_.op("pe", lambda e: e.transpose(out_, in_, IDN[:, :]), list(r) + [bIDN], w)

        def tt(q, out_, in0, in1, op, r=(), w=()):
            S_.op(q, lambda e: e.tensor_tensor(out=out_, in0=in0, in1=in1, op=op), r, w)

        def ts(q, out_, in0, s1, s2, op0, op1=None, r=(), w=()):
            if op1 is None:
                S_.op(q, lambda e: e.tensor_scalar(out=out_, in0=in0, scalar1=s1, scalar2=None, op0=op0), r, w)
            else:
                S_.op(q, lambda e: e.tensor_scalar(out=out_, in0=in0, scalar1=s1, scalar2=s2, op0=op0, op1=op1), r, w)

        def stt(out_, in0, sc, in1, op0, op1, r=(), w=()):
            S_.op("dve", lambda e: e.scalar_tensor_tensor(out=out_, in0=in0, scalar=sc, in1=in1, op0=op0, op1=op1), r, w)

        def cp(q, out_, in_, r=(), w=()):
            S_.op(q, lambda e: e.tensor_copy(out=out_, in_=in_), r, w)

        def ld(q, out_, in_, key, r=(), w=()):
            S_.dma(q, lambda e, s: e.dma_start(out=out_, in_=in_).then_inc(s, 16), key, r, w)

        def v3(ap, k):
            return ap.rearrange("p (k n) -> p k n", k=k)

        def piece_src(name, j, kc, ncols):
            w_ = srcs[name]
            return w_[0].rearrange("(kc p) n -> p kc n", p=128)[:, :, j * ncols:(j + 1) * ncols]

        ring_i = [0]

        def next_slot(avoid=None):
            k = ring_i[0] % NSLOT
            if k == avoid:
                ring_i[0] += 1
                k = ring_i[0] % NSLOT
            ring_i[0] += 1
            last_slot[0] = k
            return k

        last_slot = [0]

        cast_done = set()

        def load_piece(name, j, avoid=None):
            pi = PIDX[(name, j)]
            _, _, kc, ncols = PIECES[pi]
            k = next_slot(avoid)
            n = kc * ncols
            if pi not in cast_done:
                cast_done.add(pi)
                if name == "rg":
                    def f2(e, s, k=k):
                        e.dma_start(out=v3(slots[k][:, 0:1280], 10), in_=w_rg_a[0].rearrange("h i j -> i h j")).then_inc(s, 16)
                        e.dma_start(out=v3(slots[k][:, 1280:2560], 10), in_=w_rg_x[0].rearrange("h i j -> i h j")).then_inc(s, 16)
                    S_.dma("pool", f2, "cast%d" % k, r=[], w=[bslot[k]], n=2)
                else:
                    ld("pool", v3(slots[k][:, 0:n], kc), piece_src(name, j, kc, ncols), "cast%d" % k, r=[], w=[bslot[k]])
                ld("sp", wsc[pi][:, 0:n], slots[k][:, 0:n], "st%d" % k, r=[bslot[k]], w=[bscr[pi]])
            else:
                ld("sp", slots[k][:, 0:n], wsc[pi][:, 0:n], "ld%d" % k, r=[bscr[pi]], w=[bslot[k]])
            return v3(slots[k][:, 0:n], kc), bslot[k]

        XS = XNs[1]
        bXS = bXNs[1]
        ROW_LNVB = XS[0:1, 0, 0:SW]
        ROW_BSP = XS[0:1, 1, 0:SW]
        ROW_ONE = XS[0:1, 2, 0:128]
        W1Rv = XS[0:1, 2, 128:128 + SW]
        WTF = XS[:, 3, 0:SW].rearrange("p (g t) -> p g t", g=SC)
        cl = [(IDN[:, :], ident_in[:, :]), (G2B[:, :], g2b[:, :]), (B2B[:, :], b2b[:, :]),
              (BINC[:, :], b_in_c[:, :]), (BADA[:, :], b_ada_c[:, :]), (CVW[:, :, :], convw_c[:, :, :]),
              (CVB[:, :], convb_c[:, :]), (BRA[:, :], b_rga_c[:, :]), (BRX[:, :], b_rgx_c[:, :]),
              (CC[:, :], lam_c[:, :]), (LVG[:, :], lnvg_c[:, :]), (L1G[:, :], ln1g_c[:, :]),
              (L1B[:, :], ln1b_c[:, :]), (CT[:, :, :], c_t[:, :, :]),
              (ROW_LNVB, lnvb_row[:, :]), (ROW_BSP, bsp_row[:, :])]

        def const_loads(e, s):
            for o_, i_ in cl:
                e.dma_start(out=o_, in_=i_).then_inc(s, 16)
        S_.dma("sp", const_loads, "const", r=[], w=[bC, bIDN, bG2B, bROW, bCACT] + bXS, n=len(cl))
        S_.dma("pool", lambda e, s: e.dma_start(out=ROWB[0:1, 0:SW], in_=b_in_v[:, :]).then_inc(s, 16),
               "rowb", r=[], w=[bROW])

        S_.op("dve", lambda e: e.memset(ROW_ONE, 1.0), bXS, bXS)
        S_.op("dve", lambda e: e.memset(ROWB[0:1, SW:SW + 128], 1.0), [bROW], [bROW])
        S_.op("dve", lambda e: e.memset(ONEC[:, :], 1.0), [], [bC])
        S_.op("dve", lambda e: e.memset(MHALF[:, :], -0.5), [], [bC])
        S_.op("dve", lambda e: e.memset(HST[:, :], 0.0), [], bHST)
        act(CC[:, :], CC[:, :], AF.Exp, scale=-1.0, r=[bC], w=[bC])
        act(CC[:, :], CC[:, :], AF.Ln, bias=1.0, r=[bC], w=[bC])
        ts("dve", CC[:, :], CC[:, :], -8.0, None, ALU.mult, r=[bC], w=[bC])
        ts("dve", CCH[:, :], CC[:, :], 0.5, None, ALU.mult, r=[bC], w=[bC])
        ts("dve", BRAH[:, :], BRA[:, :], 0.5, None, ALU.mult, r=[bC], w=[bC])
        ts("dve", BRXH[:, :], BRX[:, :], 0.5, None, ALU.mult, r=[bC], w=[bC])
        ts("dve", BINH[:, :], BINC[:, :], 0.5, None, ALU.mult, r=[bC], w=[bC])
        ts("dve", AG1[:, :], L1G[:, :], ALPHA, None, ALU.mult, r=[bC], w=[bC])
        ts("dve", AB1[:, :], L1B[:, :], ALPHA, None, ALU.mult, r=[bC], w=[bC])
        act(CSG[:, :, :], CT[:, :, :], AF.Sigmoid, r=[bCACT], w=[bCACT])
        tt("dve", CACT[:, :, :], CT[:, :, :], CSG[:, :, :], ALU.mult, r=[bCACT], w=[bCACT])

        l1g_b = bass.AP(L1G[:, :].tensor, 0, [[KC, 128], [1, KC], [0, BL]])
        l1b_b = bass.AP(L1B[:, :].tensor, 0, [[KC, 128], [1, KC], [0, BL]])
        mod_pending = list(range(12))

        def mod_step(avoid=None):
            if not mod_pending:
                return
            j = mod_pending.pop(0)
            k = next_slot(avoid)
            srcv = w_ada[0].rearrange("(kc p) n -> p kc n", p=128)[:, :, j * 512:(j + 1) * 512]
            ld("pool", v3(slots[k][:, 0:4096], 8), srcv, "cast%d" % k, r=[], w=[bslot[k]])
            wv = v3(slots[k][:, 0:4096], 8)
            for f in range(4):
                fi = j * 4 + f
                bk, bb_ = nb()
                for kc in range(KC):
                    mm(bk[:, 0:BL], wv[:, kc, f * 128:(f + 1) * 128], CACT[:, kc, :], kc == 0, kc == KC - 1,
                       r=[bslot[k], bCACT], w=[bb_])
                act(MOD[:, fi, :], bk[:, 0:BL], AF.Identity, bias=BADA[:, fi:fi + 1], r=[bb_, bC], w=[bMOD])
            if j == 3:
                ts("dve", S1P[:, :, :], MOD[:, 8:16, :], 1.0, None, ALU.add, r=[bMOD], w=[bDER])
            elif j == 5:
                ts("dve", G1P[:, :, :], MOD[:, 16:24, :], 1.0, 0.5, ALU.add, ALU.mult, r=[bMOD], w=[bDER])
            elif j == 9:
                ts("dve", A2[:, :, :], MOD[:, 32:40, :], 1.0, None, ALU.add, r=[bMOD], w=[bDER])
                tt("dve", B2[:, :, :], A2[:, :, :], l1b_b, ALU.mult, r=[bDER, bC], w=[bDER])
                tt("dve", B2[:, :, :], B2[:, :, :], MOD[:, 24:32, :], ALU.add, r=[bDER, bMOD], w=[bDER])
                tt("dve", A2[:, :, :], A2[:, :, :], l1g_b, ALU.mult, r=[bDER, bC], w=[bDER])
            elif j == 11:
                ts("dve", G2P[:, :, :], MOD[:, 40:48, :], 1.0, None, ALU.add, r=[bMOD], w=[bDER])

        for _ in range(4):
            mod_step()

        for g in range(SC):
            t_, tb_ = rF.next()
            ld("sp", t_[:, 0:128], w_sp[0, g], "wsp%d" % rF.k, r=[], w=[tb_])
            bk, bb_ = nb()
            tr(bk[:, 0:128], t_[:, 0:128], r=[tb_], w=[bb_])
            cp("dve", WTF[:, g, :], bk[:, 0:128], r=[bb_], w=bXS)
            S_.op("dve", lambda e, g=g: e.memset(WTF[64:128, g, 0:64], 0.0), bXS, bXS)
            cp("dve", WTB[:, g, :], WTF[:, g, :], r=bXS, w=[bWT])
            bk, bb_ = nb()
            mm(bk[0:1, 0:128], ONEC[:, 0:1], WTF[:, g, :], True, True, r=bXS + [bC], w=[bb_])
            cp("dve", W1Rv[0:1, g * 128:(g + 1) * 128], bk[0:1, 0:128], r=[bb_], w=bXS)
            bk, bb_ = nb()
            mm(bk[:, 0:128], ROW_LNVB[0:1, g * 128:(g + 1) * 128], W1Rv[0:1, g * 128:(g + 1) * 128], True, False,
               r=bXS, w=[bb_])
            mm(bk[:, 0:128], ROW_ONE, ROW_BSP[0:1, g * 128:(g + 1) * 128], False, True, r=bXS, w=[bb_])
            cp("dve", BB[:, g, :], bk[:, 0:128], r=[bb_], w=[bWT])

        NCHUNK = BL * NCH

        def ln_stats_a(XN, bXN):
            for t4 in range(4):
                for h in range(2):
                    S_.op("dve", lambda e, t4=t4, h=h: e.bn_stats(out=ST6[:, t4, h, :], in_=XN[:, t4, h * 512:(h + 1) * 512]),
                          [bXN[t4]], [bSTAT])
                S_.op("dve", lambda e, t4=t4: e.bn_aggr(out=MV[:, t4, :], in_=ST6[:, t4, :, :]), [bSTAT], [bSTAT])
            ts("dve", VE[:, :], MV[:, :, 1], LN_EPS, None, ALU.add, r=[bSTAT], w=[bSTAT])
            tt("pool", RS[:, :], VE[:, :], MHALF[:, :], ALU.pow, r=[bSTAT, bC], w=[bSTAT])
            stt(NMR[:, :], MV[:, :, 0], -1.0, RS[:, :], ALU.mult, ALU.mult, r=[bSTAT], w=[bSTAT])

        def ln_norm(XN, bXN):
            for t4 in range(4):
                act(XN[:, t4, :], XN[:, t4, :], AF.Identity, bias=NMR[:, t4:t4 + 1], scale=RS[:, t4:t4 + 1],
                    r=[bXN[t4], bSTAT], w=[bXN[t4]])

        def xload(ch):
            XN, bXN = XNs[ch % 2], bXNs[ch % 2]
            tok0 = ch * T
            ld("pool", XN[:, :, :], x[tok0:tok0 + T, :].rearrange("(t p) d -> p t d", p=128), "xin%d" % (ch % 2),
               r=[], w=bXN)

        def inproj_chunk(wv, wb, j, f):
            cc = j * 4 + f
            bk, bb_ = nb()
            for kc in range(KC):
                mm(bk[:, :], wv[:, kc, f * 128:(f + 1) * 128], hT[:, kc, :], kc == 0, kc == KC - 1,
                   r=[wb, bhT[kc]], w=[bb_])
            bias = BINC[:, cc:cc + 1]
            if cc < 10:
                act(XL[:, cc, 4:4 + T], bk[:, :], AF.Identity, bias=bias, r=[bb_, bC], w=[bXL[cc]])
            elif cc < 20:
                act(GG[:, cc - 10, :], bk[:, :], AF.Gelu_apprx_tanh, bias=bias, r=[bb_, bC], w=[bGG[cc - 10]])
            elif cc < 26:
                act(UG[:, cc - 20, :], bk[:, :], AF.Gelu_apprx_tanh, bias=bias, r=[bb_, bC], w=[bUG[cc - 20]])
            else:
                act(SIG[:, cc - 32, :], bk[:, :], AF.Tanh, bias=BINH[:, cc:cc + 1], scale=0.5,
                    r=[bb_, bC], w=[bSIG[cc - 32]])

        def Ma_t(ch):
            XN, bXN = XNs[ch % 2], bXNs[ch % 2]
            b = ch // NCH
            for kc in range(KC):
                bk, bb_ = nb()
                for t4 in range(4):
                    tr(bk[:, t4 * 128:(t4 + 1) * 128], XN[:, t4, kc * 128:(kc + 1) * 128], r=[bXN[t4]], w=[bb_])
                act(hT[:, kc, :], bk[:, :], AF.Identity, bias=MOD[:, kc, b:b + 1], scale=S1P[:, kc, b:b + 1],
                    r=[bb_, bMOD, bDER], w=[bhT[kc]])

        def Ma_p(ch, pieces=(0, 1, 2)):
            ci = ch % NCH
            if ci == 0 and 0 in pieces:
                S_.op("dve", lambda e: e.memset(XL[:, :, 0:4], 0.0), [], bXL)
            for j in pieces:
                wv, wb = load_piece("w_in", j)
                for f in range(4):
                    inproj_chunk(wv, wb, j, f)

        def conv(c):
            ta, tab = rF.next()
            ts("dve", ta[:, :], XL[:, c, 4:4 + T], CVW[:, c, 3:4], CVB[:, c:c + 1], ALU.mult, ALU.add,
               r=[bXL[c], bC], w=[tab])
            stt(ta[:, :], XL[:, c, 3:3 + T], CVW[:, c, 2:3], ta[:, :], ALU.mult, ALU.add, r=[bXL[c], tab, bC], w=[tab])
            stt(ta[:, :], XL[:, c, 2:2 + T], CVW[:, c, 1:2], ta[:, :], ALU.mult, ALU.add, r=[bXL[c], tab, bC], w=[tab])
            stt(BIG[:, c, :], XL[:, c, 1:1 + T], CVW[:, c, 0:1], ta[:, :], ALU.mult, ALU.add,
                r=[bXL[c], tab, bC], w=[bBIG[c]])
            cp("dve", XL[:, c, 0:4], XL[:, c, T:T + 4], r=[bXL[c]], w=[bXL[c]])

        def lru_gates(gv, gb, c):
            bkr, bbr = nb()
            mm(bkr[:, :], gv[:, c, :], BIG[:, c, :], True, True, r=[gb, bBIG[c]], w=[bbr])
            bki, bbi = nb()
            mm(bki[:, :], gv[:, 10 + c, :], BIG[:, c, :], True, True, r=[gb, bBIG[c]], w=[bbi])
            act(AR[:, c, :], bkr[:, :], AF.Tanh, bias=BRAH[:, c:c + 1], scale=0.5, r=[bbr, bC], w=[bAR[c]])
            act(BIG[:, 10 + c, :], bki[:, :], AF.Tanh, bias=BRXH[:, c:c + 1], scale=0.5, r=[bbi, bC], w=[bBIG[10 + c]])
            act(AR[:, c, :], AR[:, c, :], AF.Exp, bias=CCH[:, c:c + 1], scale=CCH[:, c:c + 1], r=[bAR[c], bC], w=[bAR[c]])
            tq, tqb = rF.next()
            stt(tq[:, :], AR[:, c, :], -1.0, AR[:, c, :], ALU.mult, ALU.mult, r=[bAR[c]], w=[tqb])
            ts("dve", BIG[:, 20 + c, :], tq[:, :], 1.0, None, ALU.add, r=[tqb], w=[bBIG[20 + c]])

        def sgu(ch):
            wv6, wb6 = load_piece("w_in", 6)
            wv, wb = load_piece("w_in", 7)
            ones_b = ROWB[0:1, SW:SW + 128]
            for t4 in range(4):
                b6, bb6 = nb()
                for kc in range(KC):
                    mm(b6[:, 0:256], hT[:, kc, t4 * 128:(t4 + 1) * 128], wv6[:, kc, 256:512], kc == 0, False,
                       r=[wb6, bhT[kc]], w=[bb6])
                mm(b6[:, 0:256], ones_b, ROWB[0:1, 0:256], False, True, r=[bROW], w=[bb6])
                b7, bb7 = nb()
                for kc in range(KC):
                    mm(b7[:, 0:512], hT[:, kc, t4 * 128:(t4 + 1) * 128], wv[:, kc, 0:512], kc == 0, False,
                       r=[wb, bhT[kc]], w=[bb7])
                mm(b7[:, 0:512], ones_b, ROWB[0:1, 256:768], False, True, r=[bROW], w=[bb7])
                vt, vb_ = rV.next()
                act(vt[:, 0:256], b6[:, 0:256], AF.Gelu_apprx_tanh, r=[bb6], w=[vb_])
                act(vt[:, 256:768], b7[:, 0:512], AF.Gelu_apprx_tanh, r=[bb7], w=[vb_])
                S_.op("dve", lambda e, vt=vt: e.bn_stats(out=ST6V[:, 0, :], in_=vt[:, 0:384]), [vb_], [bSTATV])
                S_.op("dve", lambda e, vt=vt: e.bn_stats(out=ST6V[:, 1, :], in_=vt[:, 384:768]), [vb_], [bSTATV])
                S_.op("dve", lambda e: e.bn_aggr(out=MVV[:, :], in_=ST6V[:, :, :]), [bSTATV], [bSTATV])
                ts("dve", VEV[:, 0:1], MVV[:, 1:2], LN_EPS, None, ALU.add, r=[bSTATV], w=[bSTATV])
                tt("pool", RSV[:, 0:1], VEV[:, 0:1], MHALF[:, 0:1], ALU.pow, r=[bSTATV, bC], w=[bSTATV])
                stt(NMRV[:, 0:1], MVV[:, 0:1], -1.0, RSV[:, 0:1], ALU.mult, ALU.mult, r=[bSTATV], w=[bSTATV])
                act(VN[:, t4, :], vt[:, :], AF.Identity, bias=NMRV[:, 0:1], scale=RSV[:, 0:1],
                    r=[vb_, bSTATV], w=[bVN[t4]])
            wv5, wb5 = load_piece("w_in", 5)
            for f in range(4):
                inproj_chunk(wv5, wb5, 5, f)
            for f in range(2):
                inproj_chunk(wv6, wb6, 6, f)
            for g in range(SC):
                bk, bb_ = nb()
                for t4 in range(4):
                    mm(bk[:, t4 * 128:(t4 + 1) * 128], VN[:, t4, g * 128:(g + 1) * 128], WTB[:, g, :], True, True,
                       r=[bVN[t4], bWT], w=[bb_])
                tf, tfb = rF.next()
                bbv = bass.AP(BB[:, :, :].tensor, g * 128, [[SC * 128, 128], [0, 4], [1, 128]])
                stt(v3(tf[:, :], 4), v3(bk[:, :], 4), LVG[:, g:g + 1], bbv, ALU.mult, ALU.add,
                    r=[bb_, bC, bWT], w=[tfb])
                tt("dve", UG[:, g, :], tf[:, :], UG[:, g, :], ALU.mult, r=[tfb, bUG[g]], w=[bUG[g]])


        def Mb(ch, hook1=None, hook2=None, hook3=None):
            XN, bXN = XNs[ch % 2], bXNs[ch % 2]
            b = ch // NCH
            ci = ch % NCH
            if hook1 is not None:
                hook1()
            for c in range(LC):
                conv(c)
            for j in (3, 4):
                wv, wb = load_piece("w_in", j)
                for f in range(4):
                    inproj_chunk(wv, wb, j, f)
                mod_step()
            if hook2 is not None:
                hook2()
            gv, gb = load_piece("rg", 0)
            k_rg = last_slot[0]
            n = 0
            for j in (8, 9, 10, 11):
                wv, wb = load_piece("w_in", j, avoid=k_rg)
                for f in range(4):
                    inproj_chunk(wv, wb, j, f)
                    if 1 <= n <= LC:
                        lru_gates(gv, gb, n - 1)
                    n += 1
                mod_step(avoid=k_rg)
            for c in range(LC):
                act(BIG[:, 20 + c, :], BIG[:, 20 + c, :], AF.Sqrt, scale=0.25, r=[bBIG[20 + c]], w=[bBIG[20 + c]])
            for c in range(LC):
                stt(BIG[:, 20 + c, :], BIG[:, 10 + c, :], 1.0, BIG[:, 20 + c, :], ALU.add, ALU.mult,
                    r=[bBIG[20 + c], bBIG[10 + c]], w=[bBIG[20 + c]])
            for c in range(LC):
                tt("dve", BIG[:, c, :], BIG[:, 20 + c, :], BIG[:, c, :], ALU.mult,
                   r=[bBIG[20 + c], bBIG[c]], w=[bBIG[c]])
            for c in range(LC):
                th, thb = rF.next()
                init = 0.0 if ci == 0 else HST[:, c:c + 1]
                S_.op("dve", lambda e, th=th, c=c, init=init: e.tensor_tensor_scan(
                    out=th[:, :], data0=AR[:, c, :], data1=BIG[:, c, :], initial=init, op0=ALU.mult, op1=ALU.add),
                    [bAR[c], bBIG[c], bHST[c]], [thb])
                cp("dve", HST[:, c:c + 1], th[:, T - 1:T], r=[thb], w=[bHST[c]])
                tt("dve", BIG[:, 10 + c, :], th[:, :], GG[:, c, :], ALU.mult, r=[thb, bGG[c]], w=[bBIG[10 + c]])
            ybank = []
            psg = [load_piece("w_o_sgu", 0), load_piece("w_o_sgu", 1)]
            t2s = []
            for oc in range(KC):
                jj, f = oc // 4, oc % 4
                wv2, wb2 = psg[jj]
                bkb, bbb = nb()
                for g in range(SC):
                    mm(bkb[:, :], wv2[:, g, f * 128:(f + 1) * 128], UG[:, g, :], g == 0, g == SC - 1,
                       r=[wb2, bUG[g]], w=[bbb])
                act(MG[:, oc, :], bkb[:, :], AF.Copy, r=[bbb], w=[bMG[oc]])
            if hook3 is not None:
                hook3()
            for oc in range(KC):
                stt(MG[:, oc, :], SIG[:, 8 + oc, :], 1.0, MG[:, oc, :], ALU.add, ALU.mult,
                    r=[bSIG[8 + oc], bMG[oc]], w=[bMG[oc]])
            pa = {}
            for oc in range(KC):
                jj, f = oc // 2, oc % 2
                if jj not in pa:
                    pa[jj] = load_piece("w_o_lru", jj)
                wv, wb = pa[jj]
                bka, bba = nb()
                for c in range(LC):
                    mm(bka[:, :], wv[:, c, f * 128:(f + 1) * 128], BIG[:, 10 + c, :], c == 0, c == LC - 1,
                       r=[wb, bBIG[10 + c]], w=[bba])
                t1, t1b = rF.next()
                stt(t1[:, :], SIG[:, oc, :], 1.0, bka[:, :], ALU.add, ALU.mult, r=[bba, bSIG[oc]], w=[t1b])
                tt("dve", MG[:, oc, :], t1[:, :], MG[:, oc, :], ALU.add, r=[t1b, bMG[oc]], w=[bMG[oc]])

            prev = None
            for oc in range(KC + 1):
                cur = None
                if oc < KC:
                    jj, f = oc // 4, oc % 4
                    if f == 0:
                        wv, wb = load_piece("w_out", jj)
                    bk, bb_ = nb()
                    for kc in range(KC):
                        mm(bk[:, :], wv[:, kc, f * 128:(f + 1) * 128], MG[:, kc, :], kc == 0, kc == KC - 1,
                           r=[wb, bMG[kc]], w=[bb_])
                    tm, tmb = rF.next()
                    act(tm[:, :], bk[:, :], AF.Identity, scale=G1P[:, oc, b:b + 1], r=[bb_, bDER], w=[tmb])
                    cur = (oc, tm, tmb)
                if prev is not None:
                    po, tm_, tmb_ = prev
                    bk2, bb2 = nb()
                    for t4 in range(4):
                        tr(bk2[:, t4 * 128:(t4 + 1) * 128], tm_[:, t4 * 128:(t4 + 1) * 128], r=[tmb_], w=[bb2])
                    stt(XN[:, :, po * 128:(po + 1) * 128], XN[:, :, po * 128:(po + 1) * 128], ALPHA, v3(bk2[:, :], 4),
                        ALU.mult, ALU.add, r=[bb2] + bXN, w=bXN)
                prev = cur
            ln_stats_a(XN, bXN)

        def Fa(ch):
            XN, bXN = XNs[ch % 2], bXNs[ch % 2]
            b = ch // NCH
            for kc in range(KC):
                bk, bb_ = nb()
                for t4 in range(4):
                    tr(bk[:, t4 * 128:(t4 + 1) * 128], XN[:, t4, kc * 128:(kc + 1) * 128], r=[bXN[t4]], w=[bb_])
                act(H2[:, kc, :], bk[:, :], AF.Identity, bias=B2[:, kc, b:b + 1], scale=A2[:, kc, b:b + 1],
                    r=[bb_, bDER], w=[bH2[kc]])
                act(AR[:, kc, :], bk[:, :], AF.Identity, bias=AB1[:, kc:kc + 1], scale=AG1[:, kc:kc + 1],
                    r=[bb_, bC], w=[bAR[kc]])

        def Fb(ch):
            XN, bXN = XNs[ch % 2], bXNs[ch % 2]
            b = ch // NCH
            tok0 = ch * T
            for j in range(8):
                wv, wb = load_piece("w_up", j)
                for f in range(4):
                    fc = j * 4 + f
                    bk, bb_ = nb()
                    for kc in range(KC):
                        mm(bk[:, :], wv[:, kc, f * 128:(f + 1) * 128], H2[:, kc, :], kc == 0, kc == KC - 1,
                           r=[wb, bH2[kc]], w=[bb_])
                    tr_, trb = rF.next()
                    act(tr_[:, :], bk[:, :], AF.Relu, r=[bb_], w=[trb])
                    tt("dve", BIG[:, fc, :], tr_[:, :], tr_[:, :], ALU.mult, r=[trb], w=[bBIG[fc]])
                mod_step()
            prev = None
            for oc in range(KC + 1):
                cur = None
                if oc < KC:
                    wv, wb = load_piece("w_down", oc)
                    bk, bb_ = nb()
                    for fc in range(FC):
                        mm(bk[:, :], wv[:, fc, :], BIG[:, fc, :], fc == 0, fc == FC - 1, r=[wb, bBIG[fc]], w=[bb_])
                    stt(AR[:, oc, :], bk[:, :], G2P[:, oc, b:b + 1], AR[:, oc, :], ALU.mult, ALU.add,
                        r=[bb_, bDER, bAR[oc]], w=[bAR[oc]])
                    cur = oc
                if prev is not None:
                    bk2, bb2 = nb()
                    for t4 in range(4):
                        tr(bk2[:, t4 * 128:(t4 + 1) * 128], AR[:, prev, t4 * 128:(t4 + 1) * 128], r=[bAR[prev]], w=[bb2])
                    act(XN[:, :, prev * 128:(prev + 1) * 128], v3(bk2[:, :], 4), AF.Copy, r=[bb2], w=bXN)
                prev = cur

        def Fb_tail_a(ch):
            ln_stats_a(XNs[ch % 2], bXNs[ch % 2])

        def Fb_tail_b(ch):
            XN, bXN = XNs[ch % 2], bXNs[ch % 2]
            tok0 = ch * T
            ln_norm(XN, bXN)
            for t4 in range(4):
                tt("dve", XN[:, t4, :], XN[:, t4, :], G2B[:, :], ALU.mult, r=[bXN[t4], bG2B], w=[bXN[t4]])
                tt("dve", XN[:, t4, :], XN[:, t4, :], B2B[:, :], ALU.add, r=[bXN[t4], bG2B], w=[bXN[t4]])
            ld("pool", out[tok0:tok0 + T, :].rearrange("(t p) d -> p t d", p=128), XN[:, :, :], "xout%d" % (ch % 2),
               r=bXN, w=[])

        xload(0)
        xload(1)
        Ma_t(0)
        Ma_p(0)
        sgu(0)
        for ch in range(NCHUNK):
            def hook1(ch=ch):
                if ch >= 1:
                    Fb_tail_a(ch - 1)

            def hook2(ch=ch):
                if ch >= 1:
                    Fb_tail_b(ch - 1)
                    if ch + 1 < NCHUNK:
                        xload(ch + 1)

            def hook3(ch=ch):
                if ch + 1 < NCHUNK:
                    Ma_t(ch + 1)
                    Ma_p(ch + 1)
            Mb(ch, hook1, hook2, hook3)
            if ch + 1 < NCHUNK:
                sgu(ch + 1)
            ln_norm(XNs[ch % 2], bXNs[ch % 2])
            Fa(ch)
            Fb(ch)
        Fb_tail_a(NCHUNK - 1)
        Fb_tail_b(NCHUNK - 1)

        S_.op("pool", lambda e: e.memset(JUNK[:, :], 0.0), [], bXNs[0] + bXNs[1] + [bJ])
        S_.emit(nc)
    return nc


_NC_CACHE = {}


def _col(v, n):
    return np.ascontiguousarray(np.asarray(v, np.float32).reshape(n, 128).T)


def kernel(x, c, w_ada, b_ada, w_in, b_in, w_conv, b_conv, w_rg_a, b_rg_a, w_rg_x, b_rg_x,
           lru_lambda, w_sp, b_sp, ln_v_g, ln_v_b, w_o_lru, w_o_sgu, w_out, ln1_g, ln1_b,
           w_up, w_down, ln2_g, ln2_b):
    f = lambda a: np.ascontiguousarray(np.asarray(a, dtype=np.float32))
    x = f(x)
    c = f(c)
    if "nc" not in _NC_CACHE:
        _NC_CACHE["nc"] = build_nc()
    nc = _NC_CACHE["nc"]
    shared = {
        "w_ada": f(w_ada), "b_ada_c": _col(b_ada[0], 48),
        "w_in": f(w_in), "b_in_c": _col(b_in[0], 48),
        "b_in_v": f(np.asarray(b_in)[0:1, 3328:4096]),
        "convw_c": np.ascontiguousarray(np.asarray(w_conv, np.float32)[0].reshape(4, LC, 128).transpose(2, 1, 0)),
        "convb_c": _col(b_conv[0], LC),
        "w_rg_a": f(w_rg_a), "w_rg_x": f(w_rg_x),
        "b_rga_c": _col(b_rg_a[0], LC), "b_rgx_c": _col(b_rg_x[0], LC), "lam_c": _col(lru_lambda[0], LC),
        "w_sp": f(w_sp), "bsp_row": f(np.asarray(b_sp)[0].reshape(1, SW)),
        "lnvg_c": _col(ln_v_g[0], SC), "lnvb_row": f(np.asarray(ln_v_b)[0:1]),
        "w_o_lru": f(w_o_lru), "w_o_sgu": f(w_o_sgu), "w_out": f(w_out), "w_up": f(w_up), "w_down": f(w_down),
        "ln1g_c": _col(ln1_g[0], KC), "ln1b_c": _col(ln1_b[0], KC),
        "g2b": np.ascontiguousarray(np.broadcast_to(np.asarray(ln2_g, np.float32)[0:1], (128, D))),
        "b2b": np.ascontiguousarray(np.broadcast_to(np.asarray(ln2_b, np.float32)[0:1], (128, D))),
        "ident": np.eye(128, dtype=np.float32),
    }
    in_maps = []
    for i in range(NCORES):
        m = dict(shared)
        m["x"] = x[i * BL:(i + 1) * BL].reshape(BL * S, D)
        ci = c[i * BL:(i + 1) * BL]
        m["c_t"] = np.ascontiguousarray(ci.reshape(BL, KC, 128).transpose(2, 1, 0))
        in_maps.append(m)
    res = run_bass_kernel_spmd(nc, in_maps, core_ids=list(range(NCORES)))
    outs = [np.asarray(r["out"], dtype=np.float32).reshape(BL, S, D) for r in res.results]
    return np.concatenate(outs, axis=0)
```

```python
import contextlib
import numpy as np
import concourse.bass as bass
import concourse.mybir as mybir
from concourse.bass_utils import run_bass_kernel_spmd

F32 = mybir.dt.float32
BF16 = mybir.dt.bfloat16
AF = mybir.ActivationFunctionType
ALU = mybir.AluOpType

NCORES = 8
D = 1024
S = 2048
BL = 4
T = 512
NCH = S // T
LW = 1280
LC = LW // 128
SW = 768
SC = SW // 128
DFF = 4096
FC = DFF // 128
KC = D // 128
ALPHA = 2.0 ** 0.25
LN_EPS = 1e-5
SLOTW = 4096
NSLOT = 3

COMPUTE = ("pe", "act", "dve", "pool")
QUEUES = ("pe", "act", "dve", "pool", "sp")


class Buf:
    __slots__ = ("name", "last_w", "readers")

    def __init__(self, name):
        self.name = name
        self.last_w = None
        self.readers = []


class Op:
    __slots__ = ("q", "fn", "deps", "milestone", "count", "dma_key", "dma_n")

    def __init__(self, q, fn):
        self.q = q
        self.fn = fn
        self.deps = []
        self.milestone = False
        self.count = None
        self.dma_key = None
        self.dma_n = 0


class Sched:
    def __init__(self):
        self.ops = {q: [] for q in QUEUES}
        self.dma_count = {}

    def _add_dep(self, op, prod, kind):
        if prod is None or prod is op:
            return
        if prod.dma_key is None and op.dma_key is None and prod.q == op.q:
            if op.q == "pe" or kind != "raw":
                return
        op.deps.append(prod)
        if prod.dma_key is None:
            prod.milestone = True

    def _mk(self, q, fn, reads, writes, dma_key=None):
        o = Op(q, fn)
        o.dma_key = dma_key
        for b in reads:
            self._add_dep(o, b.last_w, "raw")
        for b in writes:
            self._add_dep(o, b.last_w, "waw")
            for r in b.readers:
                self._add_dep(o, r, "war")
        for b in reads:
            b.readers.append(o)
        for b in writes:
            b.last_w = o
            b.readers = []
        self.ops[q].append(o)
        return o

    def op(self, q, fn, reads=(), writes=()):
        return self._mk(q, fn, reads, writes)

    def dma(self, q, fn, key, r=(), w=(), n=1):
        reads, writes = r, w
        o = self._mk(q, fn, reads, writes, dma_key=key)
        self.dma_count[key] = self.dma_count.get(key, 0) + n
        o.dma_n = self.dma_count[key]
        return o

    def emit(self, nc):
        with contextlib.ExitStack() as es:
            sems = {}
            for q in COMPUTE:
                sems[q] = es.enter_context(nc.semaphore("s_" + q))
            for k in self.dma_count:
                sems[("dma", k)] = es.enter_context(nc.semaphore("d_" + str(k)))
            for q in QUEUES:
                c = 0
                for o in self.ops[q]:
                    if o.dma_key is None and o.milestone:
                        c += 1
                        o.count = c
            block = es.enter_context(nc.Block())
            engs = {"pe": block.tensor, "act": block.scalar, "dve": block.vector,
                    "pool": block.gpsimd, "sp": block.sync}

            def make(q):
                def body(eng):
                    seen = {}
                    for o in self.ops[q]:
                        need = {}
                        for p in o.deps:
                            if p.dma_key is not None:
                                k = ("dma", p.dma_key)
                                v = 16 * p.dma_n
                            else:
                                k = p.q
                                v = p.count
                            if v > need.get(k, 0):
                                need[k] = v
                        for k, v in need.items():
                            if v > seen.get(k, 0):
                                eng.wait_ge(sems[k], v)
                                seen[k] = v
                        if o.dma_key is not None:
                            o.fn(eng, sems[("dma", o.dma_key)])
                        else:
                            ins = o.fn(eng)
                            if o.milestone:
                                ins.then_inc(sems[q], 1)
                return body

            for q in QUEUES:
                if self.ops[q]:
                    engs[q](make(q))


def piece_table():
    P = []
    for j in range(12):
        P.append(("w_in", j, 8, 512))
    P.append(("rg", 0, 20, 128))
    for j in range(4):
        P.append(("w_o_lru", j, 10, 256))
    for j in range(2):
        P.append(("w_o_sgu", j, 6, 512))
    for j in range(2):
        P.append(("w_out", j, 8, 512))
    for j in range(8):
        P.append(("w_up", j, 8, 512))
    for j in range(8):
        P.append(("w_down", j, 32, 128))
    return P


PIECES = piece_table()
PIDX = {(n, j): i for i, (n, j, _, _) in enumerate(PIECES)}


def build_nc():
    nc = bass.Bass("TRN2", target_bir_lowering=False)
    S_ = Sched()

    def din(name, shape, dt=F32):
        return nc.dram_tensor(name, list(shape), dt, kind="ExternalInput").ap()

    x = din("x", [BL * S, D])
    out = nc.dram_tensor("out", [BL * S, D], F32, kind="ExternalOutput").ap()
    c_t = din("c_t", [128, KC, BL])
    w_ada = din("w_ada", [1, D, 6 * D])
    b_ada_c = din("b_ada_c", [128, 48])
    w_in = din("w_in", [1, D, 6144])
    b_in_c = din("b_in_c", [128, 48])
    b_in_v = din("b_in_v", [1, SW])
    convw_c = din("convw_c", [128, LC, 4])
    convb_c = din("convb_c", [128, LC])
    w_rg_a = din("w_rg_a", [1, LC, 128, 128])
    w_rg_x = din("w_rg_x", [1, LC, 128, 128])
    b_rga_c = din("b_rga_c", [128, LC])
    b_rgx_c = din("b_rgx_c", [128, LC])
    lam_c = din("lam_c", [128, LC])
    w_sp = din("w_sp", [1, SC, 128, 128])
    bsp_row = din("bsp_row", [1, SW])
    lnvg_c = din("lnvg_c", [128, SC])
    lnvb_row = din("lnvb_row", [1, SW])
    w_o_lru = din("w_o_lru", [1, LW, D])
    w_o_sgu = din("w_o_sgu", [1, SW, D])
    w_out = din("w_out", [1, D, D])
    w_up = din("w_up", [1, D, DFF])
    w_down = din("w_down", [1, DFF, D])
    ln1g_c = din("ln1g_c", [128, KC])
    ln1b_c = din("ln1b_c", [128, KC])
    g2b = din("g2b", [128, D])
    b2b = din("b2b", [128, D])
    ident_in = din("ident", [128, 128])
    wsc = nc.dram_tensor("wsc", [len(PIECES), 128, SLOTW], BF16, kind="Internal").ap()

    srcs = {"w_in": w_in, "w_o_lru": w_o_lru, "w_o_sgu": w_o_sgu, "w_out": w_out,
            "w_up": w_up, "w_down": w_down}

    with contextlib.ExitStack() as es:
        def TS(name, shape, dt):
            return es.enter_context(nc.sbuf_tensor(name, list(shape), dt))

        def PS(name, shape, dt):
            return es.enter_context(nc.psum_tensor(name, list(shape), dt))

        XNs = [TS("XNa", [128, 4, D], F32), TS("XNb", [128, 4, D], F32)]
        bXNs = [[Buf("XNa%d" % i) for i in range(4)], [Buf("XNb%d" % i) for i in range(4)]]
        H2 = TS("H2", [128, KC, T], BF16)
        bH2 = [Buf("H2%d" % i) for i in range(KC)]
        hT = TS("hT", [128, KC, T], BF16)
        bhT = [Buf("hT%d" % i) for i in range(KC)]
        BIG = TS("BIG", [128, FC, T], BF16)
        bBIG = [Buf("BIG%d" % i) for i in range(FC)]
        XL = TS("XL", [128, LC, T + 4], BF16)
        bXL = [Buf("XL%d" % i) for i in range(LC)]
        AR = TS("AR", [128, LC, T], F32)
        bAR = [Buf("AR%d" % i) for i in range(LC)]
        GG = TS("GG", [128, LC, T], BF16)
        bGG = [Buf("GG%d" % i) for i in range(LC)]
        UG = TS("UG", [128, SC, T], BF16)
        bUG = [Buf("UG%d" % i) for i in range(SC)]
        VN = TS("VN", [128, 4, SW], BF16)
        bVN = [Buf("VN%d" % i) for i in range(4)]
        SIG = TS("SIG", [128, 16, T], BF16)
        bSIG = [Buf("SIG%d" % i) for i in range(16)]
        MG = TS("MG", [128, KC, T], BF16)
        bMG = [Buf("MG%d" % i) for i in range(KC)]
        slots = [TS("slot%d" % i, [128, SLOTW], BF16) for i in range(NSLOT)]
        bslot = [Buf("slot%d" % i) for i in range(NSLOT)]
        bscr = [Buf("scr%d" % i) for i in range(len(PIECES))]

        class Ring:
            def __init__(self, name, n, shape, dt):
                self.t = [TS("%s%d" % (name, i), shape, dt) for i in range(n)]
                self.b = [Buf("%s%d" % (name, i)) for i in range(n)]
                self.i = 0

            def next(self):
                k = self.i % len(self.t)
                self.i += 1
                self.k = k
                return self.t[k], self.b[k]

        rF = Ring("rF", 3, [128, T], F32)
        rV = Ring("rV", 1, [128, SW], F32)

        IDN = TS("IDN", [128, 128], F32)
        bIDN = Buf("IDN")
        G2B = TS("G2B", [128, D], F32)
        B2B = TS("B2B", [128, D], F32)
        bG2B = Buf("G2B")
        BINC = TS("BINC", [128, 48], F32)
        BADA = TS("BADA", [128, 48], F32)
        CVW = TS("CVW", [128, LC, 4], F32)
        CVB = TS("CVB", [128, LC], F32)
        BRA = TS("BRA", [128, LC], F32)
        BRX = TS("BRX", [128, LC], F32)
        CC = TS("CC", [128, LC], F32)
        CCH = TS("CCH", [128, LC], F32)
        BRAH = TS("BRAH", [128, LC], F32)
        BRXH = TS("BRXH", [128, LC], F32)
        BINH = TS("BINH", [128, 48], F32)
        LVG = TS("LVG", [128, SC], F32)
        L1G = TS("L1G", [128, KC], F32)
        L1B = TS("L1B", [128, KC], F32)
        AG1 = TS("AG1", [128, KC], F32)
        AB1 = TS("AB1", [128, KC], F32)
        bC = Buf("consts")
        ROWB = TS("ROWB", [1, SW + 128], BF16)
        bROW = Buf("rows")
        ONEC = TS("ONEC", [128, 1], F32)
        MHALF = TS("MHALF", [128, 4], F32)
        CT = TS("CT", [128, KC, BL], F32)
        CSG = TS("CSG", [128, KC, BL], F32)
        CACT = TS("CACT", [128, KC, BL], BF16)
        bCACT = Buf("cact")
        MOD = TS("MOD", [128, 48, BL], F32)
        bMOD = Buf("MOD")
        S1P = TS("S1P", [128, KC, BL], F32)
        G1P = TS("G1P", [128, KC, BL], F32)
        G2P = TS("G2P", [128, KC, BL], F32)
        A2 = TS("A2", [128, KC, BL], F32)
        B2 = TS("B2", [128, KC, BL], F32)
        bDER = Buf("derived")
        WTB = TS("WTB", [128, SC, 128], BF16)
        BB = TS("BB", [128, SC, 128], F32)
        bWT = Buf("WT")
        HST = TS("HST", [128, LC], F32)
        bHST = [Buf("HST%d" % i) for i in range(LC)]
        ST6 = TS("ST6", [128, 4, 2, 6], F32)
        MV = TS("MV", [128, 4, 2], F32)
        VE = TS("VE", [128, 4], F32)
        RS = TS("RS", [128, 4], F32)
        NMR = TS("NMR", [128, 4], F32)
        bSTAT = Buf("stat")
        ST6V = TS("ST6V", [128, 2, 6], F32)
        MVV = TS("MVV", [128, 2], F32)
        VEV = TS("VEV", [128, 1], F32)
        RSV = TS("RSV", [128, 1], F32)
        NMRV = TS("NMRV", [128, 1], F32)
        bSTATV = Buf("statv")
        JUNK = TS("JUNK", [128, 8], F32)
        bJ = Buf("junk")

        banks = [PS("ps%d" % i, [128, 512], F32) for i in range(8)]
        bbank = [Buf("ps%d" % i) for i in range(8)]
        bi = [0]

        def nb():
            k = bi[0] % 8
            bi[0] += 1
            return banks[k], bbank[k]

        def act(out_, in_, func, bias=None, scale=None, r=(), w=()):
            kw = {}
            if bias is not None:
                kw["bias"] = bias
            if scale is not None:
                kw["scale"] = scale
            S_.op("act", lambda e: e.activation(out=out_, in_=in_, func=func, **kw), r, w)

        def mm(out_, lhsT, rhs, start, stop, r=(), w=()):
            S_.op("pe", lambda e: e.matmul(out_, lhsT=lhsT, rhs=rhs, start=start, stop=stop), r, w)

        def tr(out_, in_, r=(), w=()):
            S_.op("pe", lambda e: e.transpose(out_, in_, IDN[:, :]), list(r) + [bIDN], w)

        def tt(q, out_, in0, in1, op, r=(), w=()):
            S_.op(q, lambda e: e.tensor_tensor(out=out_, in0=in0, in1=in1, op=op), r, w)

        def ts(q, out_, in0, s1, s2, op0, op1=None, r=(), w=()):
            if op1 is None:
                S_.op(q, lambda e: e.tensor_scalar(out=out_, in0=in0, scalar1=s1, scalar2=None, op0=op0), r, w)
            else:
                S_.op(q, lambda e: e.tensor_scalar(out=out_, in0=in0, scalar1=s1, scalar2=s2, op0=op0, op1=op1), r, w)

        def stt(out_, in0, sc, in1, op0, op1, r=(), w=()):
            S_.op("dve", lambda e: e.scalar_tensor_tensor(out=out_, in0=in0, scalar=sc, in1=in1, op0=op0, op1=op1), r, w)

        def cp(q, out_, in_, r=(), w=()):
            S_.op(q, lambda e: e.tensor_copy(out=out_, in_=in_), r, w)

        def ld(q, out_, in_, key, r=(), w=()):
            S_.dma(q, lambda e, s: e.dma_start(out=out_, in_=in_).then_inc(s, 16), key, r, w)

        def v3(ap, k):
            return ap.rearrange("p (k n) -> p k n", k=k)

        def piece_src(name, j, kc, ncols):
            w_ = srcs[name]
            return w_[0].rearrange("(kc p) n -> p kc n", p=128)[:, :, j * ncols:(j + 1) * ncols]

        ring_i = [0]

        def next_slot(avoid=None):
            k = ring_i[0] % NSLOT
            if k == avoid:
                ring_i[0] += 1
                k = ring_i[0] % NSLOT
            ring_i[0] += 1
            last_slot[0] = k
            return k

        last_slot = [0]

        cast_done = set()

        def load_piece(name, j, avoid=None):
            pi = PIDX[(name, j)]
            _, _, kc, ncols = PIECES[pi]
            k = next_slot(avoid)
            n = kc * ncols
            if pi not in cast_done:
                cast_done.add(pi)
                if name == "rg":
                    def f2(e, s, k=k):
                        e.dma_start(out=v3(slots[k][:, 0:1280], 10), in_=w_rg_a[0].rearrange("h i j -> i h j")).then_inc(s, 16)
                        e.dma_start(out=v3(slots[k][:, 1280:2560], 10), in_=w_rg_x[0].rearrange("h i j -> i h j")).then_inc(s, 16)
                    S_.dma("pool", f2, "cast%d" % k, r=[], w=[bslot[k]], n=2)
                else:
                    ld("pool", v3(slots[k][:, 0:n], kc), piece_src(name, j, kc, ncols), "cast%d" % k, r=[], w=[bslot[k]])
                ld("sp", wsc[pi][:, 0:n], slots[k][:, 0:n], "st%d" % k, r=[bslot[k]], w=[bscr[pi]])
            else:
                ld("sp", slots[k][:, 0:n], wsc[pi][:, 0:n], "ld%d" % k, r=[bscr[pi]], w=[bslot[k]])
            return v3(slots[k][:, 0:n], kc), bslot[k]

        XS = XNs[1]
        bXS = bXNs[1]
        ROW_LNVB = XS[0:1, 0, 0:SW]
        ROW_BSP = XS[0:1, 1, 0:SW]
        ROW_ONE = XS[0:1, 2, 0:128]
        W1Rv = XS[0:1, 2, 128:128 + SW]
        WTF = XS[:, 3, 0:SW].rearrange("p (g t) -> p g t", g=SC)
        cl = [(IDN[:, :], ident_in[:, :]), (G2B[:, :], g2b[:, :]), (B2B[:, :], b2b[:, :]),
              (BINC[:, :], b_in_c[:, :]), (BADA[:, :], b_ada_c[:, :]), (CVW[:, :, :], convw_c[:, :, :]),
              (CVB[:, :], convb_c[:, :]), (BRA[:, :], b_rga_c[:, :]), (BRX[:, :], b_rgx_c[:, :]),
              (CC[:, :], lam_c[:, :]), (LVG[:, :], lnvg_c[:, :]), (L1G[:, :], ln1g_c[:, :]),
              (L1B[:, :], ln1b_c[:, :]), (CT[:, :, :], c_t[:, :, :]),
              (ROW_LNVB, lnvb_row[:, :]), (ROW_BSP, bsp_row[:, :])]

        def const_loads(e, s):
            for o_, i_ in cl:
                e.dma_start(out=o_, in_=i_).then_inc(s, 16)
        S_.dma("sp", const_loads, "const", r=[], w=[bC, bIDN, bG2B, bROW, bCACT] + bXS, n=len(cl))
        S_.dma("pool", lambda e, s: e.dma_start(out=ROWB[0:1, 0:SW], in_=b_in_v[:, :]).then_inc(s, 16),
               "rowb", r=[], w=[bROW])

        S_.op("dve", lambda e: e.memset(ROW_ONE, 1.0), bXS, bXS)
        S_.op("dve", lambda e: e.memset(ROWB[0:1, SW:SW + 128], 1.0), [bROW], [bROW])
        S_.op("dve", lambda e: e.memset(ONEC[:, :], 1.0), [], [bC])
        S_.op("dve", lambda e: e.memset(MHALF[:, :], -0.5), [], [bC])
        S_.op("dve", lambda e: e.memset(HST[:, :], 0.0), [], bHST)
        act(CC[:, :], CC[:, :], AF.Exp, scale=-1.0, r=[bC], w=[bC])
        act(CC[:, :], CC[:, :], AF.Ln, bias=1.0, r=[bC], w=[bC])
        ts("dve", CC[:, :], CC[:, :], -8.0, None, ALU.mult, r=[bC], w=[bC])
        ts("dve", CCH[:, :], CC[:, :], 0.5, None, ALU.mult, r=[bC], w=[bC])
        ts("dve", BRAH[:, :], BRA[:, :], 0.5, None, ALU.mult, r=[bC], w=[bC])
        ts("dve", BRXH[:, :], BRX[:, :], 0.5, None, ALU.mult, r=[bC], w=[bC])
        ts("dve", BINH[:, :], BINC[:, :], 0.5, None, ALU.mult, r=[bC], w=[bC])
        ts("dve", AG1[:, :], L1G[:, :], ALPHA, None, ALU.mult, r=[bC], w=[bC])
        ts("dve", AB1[:, :], L1B[:, :], ALPHA, None, ALU.mult, r=[bC], w=[bC])
        act(CSG[:, :, :], CT[:, :, :], AF.Sigmoid, r=[bCACT], w=[bCACT])
        tt("dve", CACT[:, :, :], CT[:, :, :], CSG[:, :, :], ALU.mult, r=[bCACT], w=[bCACT])

        l1g_b = bass.AP(L1G[:, :].tensor, 0, [[KC, 128], [1, KC], [0, BL]])
        l1b_b = bass.AP(L1B[:, :].tensor, 0, [[KC, 128], [1, KC], [0, BL]])
        mod_pending = list(range(12))

        def mod_step(avoid=None):
            if not mod_pending:
                return
            j = mod_pending.pop(0)
            k = next_slot(avoid)
            srcv = w_ada[0].rearrange("(kc p) n -> p kc n", p=128)[:, :, j * 512:(j + 1) * 512]
            ld("pool", v3(slots[k][:, 0:4096], 8), srcv, "cast%d" % k, r=[], w=[bslot[k]])
            wv = v3(slots[k][:, 0:4096], 8)
            for f in range(4):
                fi = j * 4 + f
                bk, bb_ = nb()
                for kc in range(KC):
                    mm(bk[:, 0:BL], wv[:, kc, f * 128:(f + 1) * 128], CACT[:, kc, :], kc == 0, kc == KC - 1,
                       r=[bslot[k], bCACT], w=[bb_])
                act(MOD[:, fi, :], bk[:, 0:BL], AF.Identity, bias=BADA[:, fi:fi + 1], r=[bb_, bC], w=[bMOD])
            if j == 3:
                ts("dve", S1P[:, :, :], MOD[:, 8:16, :], 1.0, None, ALU.add, r=[bMOD], w=[bDER])
            elif j == 5:
                ts("dve", G1P[:, :, :], MOD[:, 16:24, :], 1.0, 0.5, ALU.add, ALU.mult, r=[bMOD], w=[bDER])
            elif j == 9:
                ts("dve", A2[:, :, :], MOD[:, 32:40, :], 1.0, None, ALU.add, r=[bMOD], w=[bDER])
                tt("dve", B2[:, :, :], A2[:, :, :], l1b_b, ALU.mult, r=[bDER, bC], w=[bDER])
                tt("dve", B2[:, :, :], B2[:, :, :], MOD[:, 24:32, :], ALU.add, r=[bDER, bMOD], w=[bDER])
                tt("dve", A2[:, :, :], A2[:, :, :], l1g_b, ALU.mult, r=[bDER, bC], w=[bDER])
            elif j == 11:
                ts("dve", G2P[:, :, :], MOD[:, 40:48, :], 1.0, None, ALU.add, r=[bMOD], w=[bDER])

        for _ in range(4):
            mod_step()

        for g in range(SC):
            t_, tb_ = rF.next()
            ld("sp", t_[:, 0:128], w_sp[0, g], "wsp%d" % rF.k, r=[], w=[tb_])
            bk, bb_ = nb()
            tr(bk[:, 0:128], t_[:, 0:128], r=[tb_], w=[bb_])
            cp("dve", WTF[:, g, :], bk[:, 0:128], r=[bb_], w=bXS)
            S_.op("dve", lambda e, g=g: e.memset(WTF[64:128, g, 0:64], 0.0), bXS, bXS)
            cp("dve", WTB[:, g, :], WTF[:, g, :], r=bXS, w=[bWT])
            bk, bb_ = nb()
            mm(bk[0:1, 0:128], ONEC[:, 0:1], WTF[:, g, :], True, True, r=bXS + [bC], w=[bb_])
            cp("dve", W1Rv[0:1, g * 128:(g + 1) * 128], bk[0:1, 0:128], r=[bb_], w=bXS)
            bk, bb_ = nb()
            mm(bk[:, 0:128], ROW_LNVB[0:1, g * 128:(g + 1) * 128], W1Rv[0:1, g * 128:(g + 1) * 128], True, False,
               r=bXS, w=[bb_])
            mm(bk[:, 0:128], ROW_ONE, ROW_BSP[0:1, g * 128:(g + 1) * 128], False, True, r=bXS, w=[bb_])
            cp("dve", BB[:, g, :], bk[:, 0:128], r=[bb_], w=[bWT])

        NCHUNK = BL * NCH

        def ln_stats_a(XN, bXN):
            for t4 in range(4):
                for h in range(2):
                    S_.op("dve", lambda e, t4=t4, h=h: e.bn_stats(out=ST6[:, t4, h, :], in_=XN[:, t4, h * 512:(h + 1) * 512]),
                          [bXN[t4]], [bSTAT])
                S_.op("dve", lambda e, t4=t4: e.bn_aggr(out=MV[:, t4, :], in_=ST6[:, t4, :, :]), [bSTAT], [bSTAT])
            ts("dve", VE[:, :], MV[:, :, 1], LN_EPS, None, ALU.add, r=[bSTAT], w=[bSTAT])
            tt("pool", RS[:, :], VE[:, :], MHALF[:, :], ALU.pow, r=[bSTAT, bC], w=[bSTAT])
            stt(NMR[:, :], MV[:, :, 0], -1.0, RS[:, :], ALU.mult, ALU.mult, r=[bSTAT], w=[bSTAT])

        def ln_norm(XN, bXN):
            for t4 in range(4):
                act(XN[:, t4, :], XN[:, t4, :], AF.Identity, bias=NMR[:, t4:t4 + 1], scale=RS[:, t4:t4 + 1],
                    r=[bXN[t4], bSTAT], w=[bXN[t4]])

        def xload(ch):
            XN, bXN = XNs[ch % 2], bXNs[ch % 2]
            tok0 = ch * T
            ld("pool", XN[:, :, :], x[tok0:tok0 + T, :].rearrange("(t p) d -> p t d", p=128), "xin%d" % (ch % 2),
               r=[], w=bXN)

        def inproj_chunk(wv, wb, j, f):
            cc = j * 4 + f
            bk, bb_ = nb()
            for kc in range(KC):
                mm(bk[:, :], wv[:, kc, f * 128:(f + 1) * 128], hT[:, kc, :], kc == 0, kc == KC - 1,
                   r=[wb, bhT[kc]], w=[bb_])
            bias = BINC[:, cc:cc + 1]
            if cc < 10:
                act(XL[:, cc, 4:4 + T], bk[:, :], AF.Identity, bias=bias, r=[bb_, bC], w=[bXL[cc]])
            elif cc < 20:
                act(GG[:, cc - 10, :], bk[:, :], AF.Gelu_apprx_tanh, bias=bias, r=[bb_, bC], w=[bGG[cc - 10]])
            elif cc < 26:
                act(UG[:, cc - 20, :], bk[:, :], AF.Gelu_apprx_tanh, bias=bias, r=[bb_, bC], w=[bUG[cc - 20]])
            else:
                act(SIG[:, cc - 32, :], bk[:, :], AF.Tanh, bias=BINH[:, cc:cc + 1], scale=0.5,
                    r=[bb_, bC], w=[bSIG[cc - 32]])

        def Ma_t(ch):
            XN, bXN = XNs[ch % 2], bXNs[ch % 2]
            b = ch // NCH
            for kc in range(KC):
                bk, bb_ = nb()
                for t4 in range(4):
                    tr(bk[:, t4 * 128:(t4 + 1) * 128], XN[:, t4, kc * 128:(kc + 1) * 128], r=[bXN[t4]], w=[bb_])
                act(hT[:, kc, :], bk[:, :], AF.Identity, bias=MOD[:, kc, b:b + 1], scale=S1P[:, kc, b:b + 1],
                    r=[bb_, bMOD, bDER], w=[bhT[kc]])

        def Ma_p(ch, pieces=(0, 1, 2)):
            ci = ch % NCH
            if ci == 0 and 0 in pieces:
                S_.op("dve", lambda e: e.memset(XL[:, :, 0:4], 0.0), [], bXL)
            for j in pieces:
                wv, wb = load_piece("w_in", j)
                for f in range(4):
                    inproj_chunk(wv, wb, j, f)

        def conv(c):
            ta, tab = rF.next()
            ts("dve", ta[:, :], XL[:, c, 4:4 + T], CVW[:, c, 3:4], CVB[:, c:c + 1], ALU.mult, ALU.add,
               r=[bXL[c], bC], w=[tab])
            stt(ta[:, :], XL[:, c, 3:3 + T], CVW[:, c, 2:3], ta[:, :], ALU.mult, ALU.add, r=[bXL[c], tab, bC], w=[tab])
            stt(ta[:, :], XL[:, c, 2:2 + T], CVW[:, c, 1:2], ta[:, :], ALU.mult, ALU.add, r=[bXL[c], tab, bC], w=[tab])
            stt(BIG[:, c, :], XL[:, c, 1:1 + T], CVW[:, c, 0:1], ta[:, :], ALU.mult, ALU.add,
                r=[bXL[c], tab, bC], w=[bBIG[c]])
            cp("dve", XL[:, c, 0:4], XL[:, c, T:T + 4], r=[bXL[c]], w=[bXL[c]])

        def lru_gates(gv, gb, c):
            bkr, bbr = nb()
            mm(bkr[:, :], gv[:, c, :], BIG[:, c, :], True, True, r=[gb, bBIG[c]], w=[bbr])
            bki, bbi = nb()
            mm(bki[:, :], gv[:, 10 + c, :], BIG[:, c, :], True, True, r=[gb, bBIG[c]], w=[bbi])
            act(AR[:, c, :], bkr[:, :], AF.Tanh, bias=BRAH[:, c:c + 1], scale=0.5, r=[bbr, bC], w=[bAR[c]])
            act(BIG[:, 10 + c, :], bki[:, :], AF.Tanh, bias=BRXH[:, c:c + 1], scale=0.5, r=[bbi, bC], w=[bBIG[10 + c]])
            act(AR[:, c, :], AR[:, c, :], AF.Exp, bias=CCH[:, c:c + 1], scale=CCH[:, c:c + 1], r=[bAR[c], bC], w=[bAR[c]])
            tq, tqb = rF.next()
            stt(tq[:, :], AR[:, c, :], -1.0, AR[:, c, :], ALU.mult, ALU.mult, r=[bAR[c]], w=[tqb])
            ts("dve", BIG[:, 20 + c, :], tq[:, :], 1.0, None, ALU.add, r=[tqb], w=[bBIG[20 + c]])

        def sgu(ch):
            wv6, wb6 = load_piece("w_in", 6)
            wv, wb = load_piece("w_in", 7)
            ones_b = ROWB[0:1, SW:SW + 128]
            for t4 in range(4):
                b6, bb6 = nb()
                for kc in range(KC):
                    mm(b6[:, 0:256], hT[:, kc, t4 * 128:(t4 + 1) * 128], wv6[:, kc, 256:512], kc == 0, False,
                       r=[wb6, bhT[kc]], w=[bb6])
                mm(b6[:, 0:256], ones_b, ROWB[0:1, 0:256], False, True, r=[bROW], w=[bb6])
                b7, bb7 = nb()
                for kc in range(KC):
                    mm(b7[:, 0:512], hT[:, kc, t4 * 128:(t4 + 1) * 128], wv[:, kc, 0:512], kc == 0, False,
                       r=[wb, bhT[kc]], w=[bb7])
                mm(b7[:, 0:512], ones_b, ROWB[0:1, 256:768], False, True, r=[bROW], w=[bb7])
                vt, vb_ = rV.next()
                act(vt[:, 0:256], b6[:, 0:256], AF.Gelu_apprx_tanh, r=[bb6], w=[vb_])
                act(vt[:, 256:768], b7[:, 0:512], AF.Gelu_apprx_tanh, r=[bb7], w=[vb_])
                S_.op("dve", lambda e, vt=vt: e.bn_stats(out=ST6V[:, 0, :], in_=vt[:, 0:384]), [vb_], [bSTATV])
                S_.op("dve", lambda e, vt=vt: e.bn_stats(out=ST6V[:, 1, :], in_=vt[:, 384:768]), [vb_], [bSTATV])
                S_.op("dve", lambda e: e.bn_aggr(out=MVV[:, :], in_=ST6V[:, :, :]), [bSTATV], [bSTATV])
                ts("dve", VEV[:, 0:1], MVV[:, 1:2], LN_EPS, None, ALU.add, r=[bSTATV], w=[bSTATV])
                tt("pool", RSV[:, 0:1], VEV[:, 0:1], MHALF[:, 0:1], ALU.pow, r=[bSTATV, bC], w=[bSTATV])
                stt(NMRV[:, 0:1], MVV[:, 0:1], -1.0, RSV[:, 0:1], ALU.mult, ALU.mult, r=[bSTATV], w=[bSTATV])
                act(VN[:, t4, :], vt[:, :], AF.Identity, bias=NMRV[:, 0:1], scale=RSV[:, 0:1],
                    r=[vb_, bSTATV], w=[bVN[t4]])
            wv5, wb5 = load_piece("w_in", 5)
            for f in range(4):
                inproj_chunk(wv5, wb5, 5, f)
            for f in range(2):
                inproj_chunk(wv6, wb6, 6, f)
            for g in range(SC):
                bk, bb_ = nb()
                for t4 in range(4):
                    mm(bk[:, t4 * 128:(t4 + 1) * 128], VN[:, t4, g * 128:(g + 1) * 128], WTB[:, g, :], True, True,
                       r=[bVN[t4], bWT], w=[bb_])
                tf, tfb = rF.next()
                bbv = bass.AP(BB[:, :, :].tensor, g * 128, [[SC * 128, 128], [0, 4], [1, 128]])
                stt(v3(tf[:, :], 4), v3(bk[:, :], 4), LVG[:, g:g + 1], bbv, ALU.mult, ALU.add,
                    r=[bb_, bC, bWT], w=[tfb])
                tt("dve", UG[:, g, :], tf[:, :], UG[:, g, :], ALU.mult, r=[tfb, bUG[g]], w=[bUG[g]])


        def Mb(ch, hook1=None, hook2=None, hook3=None):
            XN, bXN = XNs[ch % 2], bXNs[ch % 2]
            b = ch // NCH
            ci = ch % NCH
            if hook1 is not None:
                hook1()
            for c in range(LC):
                conv(c)
                if c == 4 and hook2 is not None:
                    hook2()
            for j in (3, 4):
                wv, wb = load_piece("w_in", j)
                for f in range(4):
                    inproj_chunk(wv, wb, j, f)
                mod_step()
            gv, gb = load_piece("rg", 0)
            k_rg = last_slot[0]
            n = 0
            for j in (8, 9, 10, 11):
                wv, wb = load_piece("w_in", j, avoid=k_rg)
                for f in range(4):
                    inproj_chunk(wv, wb, j, f)
                    if 6 <= n < 6 + LC:
                        lru_gates(gv, gb, n - 6)
                    n += 1
                mod_step(avoid=k_rg)
            for c in range(LC):
                act(BIG[:, 20 + c, :], BIG[:, 20 + c, :], AF.Sqrt, scale=0.25, r=[bBIG[20 + c]], w=[bBIG[20 + c]])
            for c in range(LC):
                stt(BIG[:, 20 + c, :], BIG[:, 10 + c, :], 1.0, BIG[:, 20 + c, :], ALU.add, ALU.mult,
                    r=[bBIG[20 + c], bBIG[10 + c]], w=[bBIG[20 + c]])
            for c in range(LC):
                tt("dve", BIG[:, c, :], BIG[:, 20 + c, :], BIG[:, c, :], ALU.mult,
                   r=[bBIG[20 + c], bBIG[c]], w=[bBIG[c]])
            for c in range(LC):
                th, thb = rF.next()
                init = 0.0 if ci == 0 else HST[:, c:c + 1]
                S_.op("dve", lambda e, th=th, c=c, init=init: e.tensor_tensor_scan(
                    out=th[:, :], data0=AR[:, c, :], data1=BIG[:, c, :], initial=init, op0=ALU.mult, op1=ALU.add),
                    [bAR[c], bBIG[c], bHST[c]], [thb])
                cp("dve", HST[:, c:c + 1], th[:, T - 1:T], r=[thb], w=[bHST[c]])
                tt("dve", BIG[:, 10 + c, :], th[:, :], GG[:, c, :], ALU.mult, r=[thb, bGG[c]], w=[bBIG[10 + c]])
            ybank = []
            psg = [load_piece("w_o_sgu", 0), load_piece("w_o_sgu", 1)]
            t2s = []
            for oc in range(KC):
                jj, f = oc // 4, oc % 4
                wv2, wb2 = psg[jj]
                bkb, bbb = nb()
                for g in range(SC):
                    mm(bkb[:, :], wv2[:, g, f * 128:(f + 1) * 128], UG[:, g, :], g == 0, g == SC - 1,
                       r=[wb2, bUG[g]], w=[bbb])
                act(MG[:, oc, :], bkb[:, :], AF.Copy, r=[bbb], w=[bMG[oc]])
            if hook3 is not None:
                hook3()
            for oc in range(KC):
                stt(MG[:, oc, :], SIG[:, 8 + oc, :], 1.0, MG[:, oc, :], ALU.add, ALU.mult,
                    r=[bSIG[8 + oc], bMG[oc]], w=[bMG[oc]])
            pa = {}
            for oc in range(KC):
                jj, f = oc // 2, oc % 2
                if jj not in pa:
                    pa[jj] = load_piece("w_o_lru", jj)
                wv, wb = pa[jj]
                bka, bba = nb()
                for c in range(LC):
                    mm(bka[:, :], wv[:, c, f * 128:(f + 1) * 128], BIG[:, 10 + c, :], c == 0, c == LC - 1,
                       r=[wb, bBIG[10 + c]], w=[bba])
                t1, t1b = rF.next()
                stt(t1[:, :], SIG[:, oc, :], 1.0, bka[:, :], ALU.add, ALU.mult, r=[bba, bSIG[oc]], w=[t1b])
                tt("dve", MG[:, oc, :], t1[:, :], MG[:, oc, :], ALU.add, r=[t1b, bMG[oc]], w=[bMG[oc]])

            prev = None
            for oc in range(KC + 1):
                cur = None
                if oc < KC:
                    jj, f = oc // 4, oc % 4
                    if f == 0:
                        wv, wb = load_piece("w_out", jj)
                    bk, bb_ = nb()
                    for kc in range(KC):
                        mm(bk[:, :], wv[:, kc, f * 128:(f + 1) * 128], MG[:, kc, :], kc == 0, kc == KC - 1,
                           r=[wb, bMG[kc]], w=[bb_])
                    tm, tmb = rF.next()
                    act(tm[:, :], bk[:, :], AF.Identity, scale=G1P[:, oc, b:b + 1], r=[bb_, bDER], w=[tmb])
                    cur = (oc, tm, tmb)
                if prev is not None:
                    po, tm_, tmb_ = prev
                    bk2, bb2 = nb()
                    for t4 in range(4):
                        tr(bk2[:, t4 * 128:(t4 + 1) * 128], tm_[:, t4 * 128:(t4 + 1) * 128], r=[tmb_], w=[bb2])
                    stt(XN[:, :, po * 128:(po + 1) * 128], XN[:, :, po * 128:(po + 1) * 128], ALPHA, v3(bk2[:, :], 4),
                        ALU.mult, ALU.add, r=[bb2] + bXN, w=bXN)
                prev = cur
            ln_stats_a(XN, bXN)

        def Fa(ch):
            XN, bXN = XNs[ch % 2], bXNs[ch % 2]
            b = ch // NCH
            for kc in range(KC):
                bk, bb_ = nb()
                for t4 in range(4):
                    tr(bk[:, t4 * 128:(t4 + 1) * 128], XN[:, t4, kc * 128:(kc + 1) * 128], r=[bXN[t4]], w=[bb_])
                act(H2[:, kc, :], bk[:, :], AF.Identity, bias=B2[:, kc, b:b + 1], scale=A2[:, kc, b:b + 1],
                    r=[bb_, bDER], w=[bH2[kc]])
                act(AR[:, kc, :], bk[:, :], AF.Identity, bias=AB1[:, kc:kc + 1], scale=AG1[:, kc:kc + 1],
                    r=[bb_, bC], w=[bAR[kc]])

        def Fb(ch):
            XN, bXN = XNs[ch % 2], bXNs[ch % 2]
            b = ch // NCH
            tok0 = ch * T
            for j in range(8):
                wv, wb = load_piece("w_up", j)
                for f in range(4):
                    fc = j * 4 + f
                    bk, bb_ = nb()
                    for kc in range(KC):
                        mm(bk[:, :], wv[:, kc, f * 128:(f + 1) * 128], H2[:, kc, :], kc == 0, kc == KC - 1,
                           r=[wb, bH2[kc]], w=[bb_])
                    tr_, trb = rF.next()
                    act(tr_[:, :], bk[:, :], AF.Relu, r=[bb_], w=[trb])
                    tt("dve", BIG[:, fc, :], tr_[:, :], tr_[:, :], ALU.mult, r=[trb], w=[bBIG[fc]])
                mod_step()
            prev = None
            for oc in range(KC + 1):
                cur = None
                if oc < KC:
                    wv, wb = load_piece("w_down", oc)
                    bk, bb_ = nb()
                    for fc in range(FC):
                        mm(bk[:, :], wv[:, fc, :], BIG[:, fc, :], fc == 0, fc == FC - 1, r=[wb, bBIG[fc]], w=[bb_])
                    stt(AR[:, oc, :], bk[:, :], G2P[:, oc, b:b + 1], AR[:, oc, :], ALU.mult, ALU.add,
                        r=[bb_, bDER, bAR[oc]], w=[bAR[oc]])
                    cur = oc
                if prev is not None:
                    bk2, bb2 = nb()
                    for t4 in range(4):
                        tr(bk2[:, t4 * 128:(t4 + 1) * 128], AR[:, prev, t4 * 128:(t4 + 1) * 128], r=[bAR[prev]], w=[bb2])
                    act(XN[:, :, prev * 128:(prev + 1) * 128], v3(bk2[:, :], 4), AF.Copy, r=[bb2], w=bXN)
                prev = cur

        def Fb_tail_a(ch):
            ln_stats_a(XNs[ch % 2], bXNs[ch % 2])

        def Fb_tail_b(ch):
            XN, bXN = XNs[ch % 2], bXNs[ch % 2]
            tok0 = ch * T
            ln_norm(XN, bXN)
            for t4 in range(4):
                tt("dve", XN[:, t4, :], XN[:, t4, :], G2B[:, :], ALU.mult, r=[bXN[t4], bG2B], w=[bXN[t4]])
                tt("dve", XN[:, t4, :], XN[:, t4, :], B2B[:, :], ALU.add, r=[bXN[t4], bG2B], w=[bXN[t4]])
            ld("pool", out[tok0:tok0 + T, :].rearrange("(t p) d -> p t d", p=128), XN[:, :, :], "xout%d" % (ch % 2),
               r=bXN, w=[])

        xload(0)
        xload(1)
        Ma_t(0)
        Ma_p(0)
        sgu(0)
        for ch in range(NCHUNK):
            def hook1(ch=ch):
                if ch >= 1:
                    Fb_tail_a(ch - 1)

            def hook2(ch=ch):
                if ch >= 1:
                    Fb_tail_b(ch - 1)
                    if ch + 1 < NCHUNK:
                        xload(ch + 1)

            def hook3(ch=ch):
                if ch + 1 < NCHUNK:
                    Ma_t(ch + 1)
                    Ma_p(ch + 1)
            Mb(ch, hook1, hook2, hook3)
            if ch + 1 < NCHUNK:
                sgu(ch + 1)
            ln_norm(XNs[ch % 2], bXNs[ch % 2])
            Fa(ch)
            Fb(ch)
        Fb_tail_a(NCHUNK - 1)
        Fb_tail_b(NCHUNK - 1)

        S_.op("pool", lambda e: e.memset(JUNK[:, :], 0.0), [], bXNs[0] + bXNs[1] + [bJ])
        S_.emit(nc)
    return nc


_NC_CACHE = {}


def _col(v, n):
    return np.ascontiguousarray(np.asarray(v, np.float32).reshape(n, 128).T)


def kernel(x, c, w_ada, b_ada, w_in, b_in, w_conv, b_conv, w_rg_a, b_rg_a, w_rg_x, b_rg_x,
           lru_lambda, w_sp, b_sp, ln_v_g, ln_v_b, w_o_lru, w_o_sgu, w_out, ln1_g, ln1_b,
           w_up, w_down, ln2_g, ln2_b):
    f = lambda a: np.ascontiguousarray(np.asarray(a, dtype=np.float32))
    x = f(x)
    c = f(c)
    if "nc" not in _NC_CACHE:
        _NC_CACHE["nc"] = build_nc()
    nc = _NC_CACHE["nc"]
    shared = {
        "w_ada": f(w_ada), "b_ada_c": _col(b_ada[0], 48),
        "w_in": f(w_in), "b_in_c": _col(b_in[0], 48),
        "b_in_v": f(np.asarray(b_in)[0:1, 3328:4096]),
        "convw_c": np.ascontiguousarray(np.asarray(w_conv, np.float32)[0].reshape(4, LC, 128).transpose(2, 1, 0)),
        "convb_c": _col(b_conv[0], LC),
        "w_rg_a": f(w_rg_a), "w_rg_x": f(w_rg_x),
        "b_rga_c": _col(b_rg_a[0], LC), "b_rgx_c": _col(b_rg_x[0], LC), "lam_c": _col(lru_lambda[0], LC),
        "w_sp": f(w_sp), "bsp_row": f(np.asarray(b_sp)[0].reshape(1, SW)),
        "lnvg_c": _col(ln_v_g[0], SC), "lnvb_row": f(np.asarray(ln_v_b)[0:1]),
        "w_o_lru": f(w_o_lru), "w_o_sgu": f(w_o_sgu), "w_out": f(w_out), "w_up": f(w_up), "w_down": f(w_down),
        "ln1g_c": _col(ln1_g[0], KC), "ln1b_c": _col(ln1_b[0], KC),
        "g2b": np.ascontiguousarray(np.broadcast_to(np.asarray(ln2_g, np.float32)[0:1], (128, D))),
        "b2b": np.ascontiguousarray(np.broadcast_to(np.asarray(ln2_b, np.float32)[0:1], (128, D))),
        "ident": np.eye(128, dtype=np.float32),
    }
    in_maps = []
    for i in range(NCORES):
        m = dict(shared)
        m["x"] = x[i * BL:(i + 1) * BL].reshape(BL * S, D)
        ci = c[i * BL:(i + 1) * BL]
        m["c_t"] = np.ascontiguousarray(ci.reshape(BL, KC, 128).transpose(2, 1, 0))
        in_maps.append(m)
    res = run_bass_kernel_spmd(nc, in_maps, core_ids=list(range(NCORES)))
    outs = [np.asarray(r["out"], dtype=np.float32).reshape(BL, S, D) for r in res.results]
    return np.concatenate(outs, axis=0)
```

```python
import contextlib
import numpy as np
import concourse.bass as bass
import concourse.mybir as mybir
from concourse.bass_utils import run_bass_kernel_spmd

F32 = mybir.dt.float32
BF16 = mybir.dt.bfloat16
AF = mybir.ActivationFunctionType
ALU = mybir.AluOpType

NCORES = 8
D = 1024
S = 2048
BL = 4
T = 512
NCH = S // T
LW = 1280
LC = LW // 128
SW = 768
SC = SW // 128
DFF = 4096
FC = DFF // 128
KC = D // 128
ALPHA = 2.0 ** 0.25
LN_EPS = 1e-5
SLOTW = 4096
NSLOT = 3

COMPUTE = ("pe", "act", "dve", "pool")
QUEUES = ("pe", "act", "dve", "pool", "sp")


class Buf:
    __slots__ = ("name", "last_w", "readers")

    def __init__(self, name):
        self.name = name
        self.last_w = None
        self.readers = []


class Op:
    __slots__ = ("q", "fn", "deps", "milestone", "count", "dma_key", "dma_n")

    def __init__(self, q, fn):
        self.q = q
        self.fn = fn
        self.deps = []
        self.milestone = False
        self.count = None
        self.dma_key = None
        self.dma_n = 0


class Sched:
    def __init__(self):
        self.ops = {q: [] for q in QUEUES}
        self.dma_count = {}

    def _add_dep(self, op, prod, kind):
        if prod is None or prod is op:
            return
        if prod.dma_key is None and op.dma_key is None and prod.q == op.q:
            if op.q == "pe" or kind != "raw":
                return
        op.deps.append(prod)
        if prod.dma_key is None:
            prod.milestone = True

    def _mk(self, q, fn, reads, writes, dma_key=None):
        o = Op(q, fn)
        o.dma_key = dma_key
        for b in reads:
            self._add_dep(o, b.last_w, "raw")
        for b in writes:
            self._add_dep(o, b.last_w, "waw")
            for r in b.readers:
                self._add_dep(o, r, "war")
        for b in reads:
            b.readers.append(o)
        for b in writes:
            b.last_w = o
            b.readers = []
        self.ops[q].append(o)
        return o

    def op(self, q, fn, reads=(), writes=()):
        return self._mk(q, fn, reads, writes)

    def dma(self, q, fn, key, r=(), w=(), n=1):
        reads, writes = r, w
        o = self._mk(q, fn, reads, writes, dma_key=key)
        self.dma_count[key] = self.dma_count.get(key, 0) + n
        o.dma_n = self.dma_count[key]
        return o

    def emit(self, nc):
        with contextlib.ExitStack() as es:
            sems = {}
            for q in COMPUTE:
                sems[q] = es.enter_context(nc.semaphore("s_" + q))
            for k in self.dma_count:
                sems[("dma", k)] = es.enter_context(nc.semaphore("d_" + str(k)))
            for q in QUEUES:
                c = 0
                for o in self.ops[q]:
                    if o.dma_key is None and o.milestone:
                        c += 1
                        o.count = c
            block = es.enter_context(nc.Block())
            engs = {"pe": block.tensor, "act": block.scalar, "dve": block.vector,
                    "pool": block.gpsimd, "sp": block.sync}

            def make(q):
                def body(eng):
                    seen = {}
                    for o in self.ops[q]:
                        need = {}
                        for p in o.deps:
                            if p.dma_key is not None:
                                k = ("dma", p.dma_key)
                                v = 16 * p.dma_n
                            else:
                                k = p.q
                                v = p.count
                            if v > need.get(k, 0):
                                need[k] = v
                        for k, v in need.items():
                            if v > seen.get(k, 0):
                                eng.wait_ge(sems[k], v)
                                seen[k] = v
                        if o.dma_key is not None:
                            o.fn(eng, sems[("dma", o.dma_key)])
                        else:
                            ins = o.fn(eng)
                            if o.milestone:
                                ins.then_inc(sems[q], 1)
                return body

            for q in QUEUES:
                if self.ops[q]:
                    engs[q](make(q))


def piece_table():
    P = []
    for j in range(12):
        P.append(("w_in", j, 8, 512))
    P.append(("rg", 0, 20, 128))
    for j in range(4):
        P.append(("w_o_lru", j, 10, 256))
    for j in range(2):
        P.append(("w_o_sgu", j, 6, 512))
    for j in range(2):
        P.append(("w_out", j, 8, 512))
    for j in range(8):
        P.append(("w_up", j, 8, 512))
    for j in range(8):
        P.append(("w_down", j, 32, 128))
    return P


PIECES = piece_table()
PIDX = {(n, j): i for i, (n, j, _, _) in enumerate(PIECES)}


def build_nc():
    nc = bass.Bass("TRN2", target_bir_lowering=False)
    S_ = Sched()

    def din(name, shape, dt=F32):
        return nc.dram_tensor(name, list(shape), dt, kind="ExternalInput").ap()

    x = din("x", [BL * S, D])
    out = nc.dram_tensor("out", [BL * S, D], F32, kind="ExternalOutput").ap()
    c_t = din("c_t", [128, KC, BL])
    w_ada = din("w_ada", [1, D, 6 * D])
    b_ada_c = din("b_ada_c", [128, 48])
    w_in = din("w_in", [1, D, 6144])
    b_in_c = din("b_in_c", [128, 48])
    b_in_v = din("b_in_v", [1, SW])
    convw_c = din("convw_c", [128, LC, 4])
    convb_c = din("convb_c", [128, LC])
    w_rg_a = din("w_rg_a", [1, LC, 128, 128])
    w_rg_x = din("w_rg_x", [1, LC, 128, 128])
    b_rga_c = din("b_rga_c", [128, LC])
    b_rgx_c = din("b_rgx_c", [128, LC])
    lam_c = din("lam_c", [128, LC])
    w_sp = din("w_sp", [1, SC, 128, 128])
    bsp_row = din("bsp_row", [1, SW])
    lnvg_c = din("lnvg_c", [128, SC])
    lnvb_row = din("lnvb_row", [1, SW])
    w_o_lru = din("w_o_lru", [1, LW, D])
    w_o_sgu = din("w_o_sgu", [1, SW, D])
    w_out = din("w_out", [1, D, D])
    w_up = din("w_up", [1, D, DFF])
    w_down = din("w_down", [1, DFF, D])
    ln1g_c = din("ln1g_c", [128, KC])
    ln1b_c = din("ln1b_c", [128, KC])
    g2b = din("g2b", [128, D])
    b2b = din("b2b", [128, D])
    ident_in = din("ident", [128, 128])
    wsc = nc.dram_tensor("wsc", [len(PIECES), 128, SLOTW], BF16, kind="Internal").ap()

    srcs = {"w_in": w_in, "w_o_lru": w_o_lru, "w_o_sgu": w_o_sgu, "w_out": w_out,
            "w_up": w_up, "w_down": w_down}

    with contextlib.ExitStack() as es:
        def TS(name, shape, dt):
            return es.enter_context(nc.sbuf_tensor(name, list(shape), dt))

        def PS(name, shape, dt):
            return es.enter_context(nc.psum_tensor(name, list(shape), dt))

        XNs = [TS("XNa", [128, 4, D], F32), TS("XNb", [128, 4, D], F32)]
        bXNs = [[Buf("XNa%d" % i) for i in range(4)], [Buf("XNb%d" % i) for i in range(4)]]
        H2 = TS("H2", [128, KC, T], BF16)
        bH2 = [Buf("H2%d" % i) for i in range(KC)]
        hT = TS("hT", [128, KC, T], BF16)
        bhT = [Buf("hT%d" % i) for i in range(KC)]
        BIG = TS("BIG", [128, FC, T], BF16)
        bBIG = [Buf("BIG%d" % i) for i in range(FC)]
        XL = TS("XL", [128, LC, T + 4], BF16)
        bXL = [Buf("XL%d" % i) for i in range(LC)]
        AR = TS("AR", [128, LC, T], F32)
        bAR = [Buf("AR%d" % i) for i in range(LC)]
        GG = TS("GG", [128, LC, T], BF16)
        bGG = [Buf("GG%d" % i) for i in range(LC)]
        UG = TS("UG", [128, SC, T], BF16)
        bUG = [Buf("UG%d" % i) for i in range(SC)]
        VN = TS("VN", [128, 4, SW], BF16)
        bVN = [Buf("VN%d" % i) for i in range(4)]
        SIG = TS("SIG", [128, 16, T], BF16)
        bSIG = [Buf("SIG%d" % i) for i in range(16)]
        MG = TS("MG", [128, KC, T], BF16)
        bMG = [Buf("MG%d" % i) for i in range(KC)]
        slots = [TS("slot%d" % i, [128, SLOTW], BF16) for i in range(NSLOT)]
        bslot = [Buf("slot%d" % i) for i in range(NSLOT)]
        bscr = [Buf("scr%d" % i) for i in range(len(PIECES))]

        class Ring:
            def __init__(self, name, n, shape, dt):
                self.t = [TS("%s%d" % (name, i), shape, dt) for i in range(n)]
                self.b = [Buf("%s%d" % (name, i)) for i in range(n)]
                self.i = 0

            def next(self):
                k = self.i % len(self.t)
                self.i += 1
                self.k = k
                return self.t[k], self.b[k]

        rF = Ring("rF", 3, [128, T], F32)
        rV = Ring("rV", 1, [128, SW], F32)

        IDN = TS("IDN", [128, 128], F32)
        bIDN = Buf("IDN")
        G2B = TS("G2B", [128, D], F32)
        B2B = TS("B2B", [128, D], F32)
        bG2B = Buf("G2B")
        BINC = TS("BINC", [128, 48], F32)
        BADA = TS("BADA", [128, 48], F32)
        CVW = TS("CVW", [128, LC, 4], F32)
        CVB = TS("CVB", [128, LC], F32)
        BRA = TS("BRA", [128, LC], F32)
        BRX = TS("BRX", [128, LC], F32)
        CC = TS("CC", [128, LC], F32)
        CCH = TS("CCH", [128, LC], F32)
        BRAH = TS("BRAH", [128, LC], F32)
        BRXH = TS("BRXH", [128, LC], F32)
        BINH = TS("BINH", [128, 48], F32)
        LVG = TS("LVG", [128, SC], F32)
        L1G = TS("L1G", [128, KC], F32)
        L1B = TS("L1B", [128, KC], F32)
        AG1 = TS("AG1", [128, KC], F32)
        AB1 = TS("AB1", [128, KC], F32)
        bC = Buf("consts")
        ROWB = TS("ROWB", [1, SW + 128], BF16)
        bROW = Buf("rows")
        ONEC = TS("ONEC", [128, 1], F32)
        MHALF = TS("MHALF", [128, 4], F32)
        CT = TS("CT", [128, KC, BL], F32)
        CSG = TS("CSG", [128, KC, BL], F32)
        CACT = TS("CACT", [128, KC, BL], BF16)
        bCACT = Buf("cact")
        MOD = TS("MOD", [128, 48, BL], F32)
        bMOD = Buf("MOD")
        S1P = TS("S1P", [128, KC, BL], F32)
        G1P = TS("G1P", [128, KC, BL], F32)
        G2P = TS("G2P", [128, KC, BL], F32)
        A2 = TS("A2", [128, KC, BL], F32)
        B2 = TS("B2", [128, KC, BL], F32)
        bDER = Buf("derived")
        WTB = TS("WTB", [128, SC, 128], BF16)
        BB = TS("BB", [128, SC, 128], F32)
        bWT = Buf("WT")
        HST = TS("HST", [128, LC], F32)
        bHST = [Buf("HST%d" % i) for i in range(LC)]
        ST6 = TS("ST6", [128, 4, 2, 6], F32)
        MV = TS("MV", [128, 4, 2], F32)
        VE = TS("VE", [128, 4], F32)
        RS = TS("RS", [128, 4], F32)
        NMR = TS("NMR", [128, 4], F32)
        bSTAT = Buf("stat")
        ST6V = TS("ST6V", [128, 2, 6], F32)
        MVV = TS("MVV", [128, 2], F32)
        VEV = TS("VEV", [128, 1], F32)
        RSV = TS("RSV", [128, 1], F32)
        NMRV = TS("NMRV", [128, 1], F32)
        bSTATV = Buf("statv")
        JUNK = TS("JUNK", [128, 8], F32)
        bJ = Buf("junk")

        banks = [PS("ps%d" % i, [128, 512], F32) for i in range(8)]
        bbank = [Buf("ps%d" % i) for i in range(8)]
        bi = [0]

        def nb():
            k = bi[0] % 8
            bi[0] += 1
            return banks[k], bbank[k]

        def act(out_, in_, func, bias=None, scale=None, r=(), w=()):
            kw = {}
            if bias is not None:
                kw["bias"] = bias
            if scale is not None:
                kw["scale"] = scale
            S_.op("act", lambda e: e.activation(out=out_, in_=in_, func=func, **kw), r, w)

        def mm(out_, lhsT, rhs, start, stop, r=(), w=()):
            S_.op("pe", lambda e: e.matmul(out_, lhsT=lhsT, rhs=rhs, start=start, stop=stop), r, w)

        def tr(out_, in_, r=(), w=()):
            S_.op("pe", lambda e: e.transpose(out_, in_, IDN[:, :]), list(r) + [bIDN], w)

        def tt(q, out_, in0, in1, op, r=(), w=()):
            S_.op(q, lambda e: e.tensor_tensor(out=out_, in0=in0, in1=in1, op=op), r, w)

        def ts(q, out_, in0, s1, s2, op0, op1=None, r=(), w=()):
            if op1 is None:
                S_.op(q, lambda e: e.tensor_scalar(out=out_, in0=in0, scalar1=s1, scalar2=None, op0=op0), r, w)
            else:
                S_.op(q, lambda e: e.tensor_scalar(out=out_, in0=in0, scalar1=s1, scalar2=s2, op0=op0, op1=op1), r, w)

        def stt(out_, in0, sc, in1, op0, op1, r=(), w=()):
            S_.op("dve", lambda e: e.scalar_tensor_tensor(out=out_, in0=in0, scalar=sc, in1=in1, op0=op0, op1=op1), r, w)

        def cp(q, out_, in_, r=(), w=()):
            S_.op(q, lambda e: e.tensor_copy(out=out_, in_=in_), r, w)

        def ld(q, out_, in_, key, r=(), w=()):
            S_.dma(q, lambda e, s: e.dma_start(out=out_, in_=in_).then_inc(s, 16), key, r, w)

        def v3(ap, k):
            return ap.rearrange("p (k n) -> p k n", k=k)

        def piece_src(name, j, kc, ncols):
            w_ = srcs[name]
            return w_[0].rearrange("(kc p) n -> p kc n", p=128)[:, :, j * ncols:(j + 1) * ncols]

        ring_i = [0]

        def next_slot(avoid=None):
            k = ring_i[0] % NSLOT
            if k == avoid:
                ring_i[0] += 1
                k = ring_i[0] % NSLOT
            ring_i[0] += 1
            last_slot[0] = k
            return k

        last_slot = [0]

        cast_done = set()

        def load_piece(name, j, avoid=None):
            pi = PIDX[(name, j)]
            _, _, kc, ncols = PIECES[pi]
            k = next_slot(avoid)
            n = kc * ncols
            if pi not in cast_done:
                cast_done.add(pi)
                if name == "rg":
                    def f2(e, s, k=k):
                        e.dma_start(out=v3(slots[k][:, 0:1280], 10), in_=w_rg_a[0].rearrange("h i j -> i h j")).then_inc(s, 16)
                        e.dma_start(out=v3(slots[k][:, 1280:2560], 10), in_=w_rg_x[0].rearrange("h i j -> i h j")).then_inc(s, 16)
                    S_.dma("pool", f2, "cast%d" % k, r=[], w=[bslot[k]], n=2)
                else:
                    ld("pool", v3(slots[k][:, 0:n], kc), piece_src(name, j, kc, ncols), "cast%d" % k, r=[], w=[bslot[k]])
                ld("sp", wsc[pi][:, 0:n], slots[k][:, 0:n], "st%d" % k, r=[bslot[k]], w=[bscr[pi]])
            else:
                ld("sp", slots[k][:, 0:n], wsc[pi][:, 0:n], "ld%d" % k, r=[bscr[pi]], w=[bslot[k]])
            return v3(slots[k][:, 0:n], kc), bslot[k]

        XS = XNs[1]
        bXS = bXNs[1]
        ROW_LNVB = XS[0:1, 0, 0:SW]
        ROW_BSP = XS[0:1, 1, 0:SW]
        ROW_ONE = XS[0:1, 2, 0:128]
        W1Rv = XS[0:1, 2, 128:128 + SW]
        WTF = XS[:, 3, 0:SW].rearrange("p (g t) -> p g t", g=SC)
        cl = [(IDN[:, :], ident_in[:, :]), (G2B[:, :], g2b[:, :]), (B2B[:, :], b2b[:, :]),
              (BINC[:, :], b_in_c[:, :]), (BADA[:, :], b_ada_c[:, :]), (CVW[:, :, :], convw_c[:, :, :]),
              (CVB[:, :], convb_c[:, :]), (BRA[:, :], b_rga_c[:, :]), (BRX[:, :], b_rgx_c[:, :]),
              (CC[:, :], lam_c[:, :]), (LVG[:, :], lnvg_c[:, :]), (L1G[:, :], ln1g_c[:, :]),
              (L1B[:, :], ln1b_c[:, :]), (CT[:, :, :], c_t[:, :, :]),
              (ROW_LNVB, lnvb_row[:, :]), (ROW_BSP, bsp_row[:, :])]

        def const_loads(e, s):
            for o_, i_ in cl:
                e.dma_start(out=o_, in_=i_).then_inc(s, 16)
        S_.dma("sp", const_loads, "const", r=[], w=[bC, bIDN, bG2B, bROW, bCACT] + bXS, n=len(cl))
        S_.dma("pool", lambda e, s: e.dma_start(out=ROWB[0:1, 0:SW], in_=b_in_v[:, :]).then_inc(s, 16),
               "rowb", r=[], w=[bROW])

        S_.op("dve", lambda e: e.memset(ROW_ONE, 1.0), bXS, bXS)
        S_.op("dve", lambda e: e.memset(ROWB[0:1, SW:SW + 128], 1.0), [bROW], [bROW])
        S_.op("dve", lambda e: e.memset(ONEC[:, :], 1.0), [], [bC])
        S_.op("dve", lambda e: e.memset(MHALF[:, :], -0.5), [], [bC])
        S_.op("dve", lambda e: e.memset(HST[:, :], 0.0), [], bHST)
        act(CC[:, :], CC[:, :], AF.Exp, scale=-1.0, r=[bC], w=[bC])
        act(CC[:, :], CC[:, :], AF.Ln, bias=1.0, r=[bC], w=[bC])
        ts("dve", CC[:, :], CC[:, :], -8.0, None, ALU.mult, r=[bC], w=[bC])
        ts("dve", CCH[:, :], CC[:, :], 0.5, None, ALU.mult, r=[bC], w=[bC])
        ts("dve", BRAH[:, :], BRA[:, :], 0.5, None, ALU.mult, r=[bC], w=[bC])
        ts("dve", BRXH[:, :], BRX[:, :], 0.5, None, ALU.mult, r=[bC], w=[bC])
        ts("dve", BINH[:, :], BINC[:, :], 0.5, None, ALU.mult, r=[bC], w=[bC])
        ts("dve", AG1[:, :], L1G[:, :], ALPHA, None, ALU.mult, r=[bC], w=[bC])
        ts("dve", AB1[:, :], L1B[:, :], ALPHA, None, ALU.mult, r=[bC], w=[bC])
        act(CSG[:, :, :], CT[:, :, :], AF.Sigmoid, r=[bCACT], w=[bCACT])
        tt("dve", CACT[:, :, :], CT[:, :, :], CSG[:, :, :], ALU.mult, r=[bCACT], w=[bCACT])

        l1g_b = bass.AP(L1G[:, :].tensor, 0, [[KC, 128], [1, KC], [0, BL]])
        l1b_b = bass.AP(L1B[:, :].tensor, 0, [[KC, 128], [1, KC], [0, BL]])
        mod_pending = list(range(12))

        def mod_step(avoid=None):
            if not mod_pending:
                return
            j = mod_pending.pop(0)
            k = next_slot(avoid)
            srcv = w_ada[0].rearrange("(kc p) n -> p kc n", p=128)[:, :, j * 512:(j + 1) * 512]
            ld("pool", v3(slots[k][:, 0:4096], 8), srcv, "cast%d" % k, r=[], w=[bslot[k]])
            wv = v3(slots[k][:, 0:4096], 8)
            for f in range(4):
                fi = j * 4 + f
                bk, bb_ = nb()
                for kc in range(KC):
                    mm(bk[:, 0:BL], wv[:, kc, f * 128:(f + 1) * 128], CACT[:, kc, :], kc == 0, kc == KC - 1,
                       r=[bslot[k], bCACT], w=[bb_])
                act(MOD[:, fi, :], bk[:, 0:BL], AF.Identity, bias=BADA[:, fi:fi + 1], r=[bb_, bC], w=[bMOD])
            if j == 3:
                ts("dve", S1P[:, :, :], MOD[:, 8:16, :], 1.0, None, ALU.add, r=[bMOD], w=[bDER])
            elif j == 5:
                ts("dve", G1P[:, :, :], MOD[:, 16:24, :], 1.0, 0.5, ALU.add, ALU.mult, r=[bMOD], w=[bDER])
            elif j == 9:
                ts("dve", A2[:, :, :], MOD[:, 32:40, :], 1.0, None, ALU.add, r=[bMOD], w=[bDER])
                tt("dve", B2[:, :, :], A2[:, :, :], l1b_b, ALU.mult, r=[bDER, bC], w=[bDER])
                tt("dve", B2[:, :, :], B2[:, :, :], MOD[:, 24:32, :], ALU.add, r=[bDER, bMOD], w=[bDER])
                tt("dve", A2[:, :, :], A2[:, :, :], l1g_b, ALU.mult, r=[bDER, bC], w=[bDER])
            elif j == 11:
                ts("dve", G2P[:, :, :], MOD[:, 40:48, :], 1.0, None, ALU.add, r=[bMOD], w=[bDER])

        for _ in range(4):
            mod_step()

        for g in range(SC):
            t_, tb_ = rF.next()
            ld("sp", t_[:, 0:128], w_sp[0, g], "wsp%d" % rF.k, r=[], w=[tb_])
            bk, bb_ = nb()
            tr(bk[:, 0:128], t_[:, 0:128], r=[tb_], w=[bb_])
            cp("dve", WTF[:, g, :], bk[:, 0:128], r=[bb_], w=bXS)
            S_.op("dve", lambda e, g=g: e.memset(WTF[64:128, g, 0:64], 0.0), bXS, bXS)
            cp("dve", WTB[:, g, :], WTF[:, g, :], r=bXS, w=[bWT])
            bk, bb_ = nb()
            mm(bk[0:1, 0:128], ONEC[:, 0:1], WTF[:, g, :], True, True, r=bXS + [bC], w=[bb_])
            cp("dve", W1Rv[0:1, g * 128:(g + 1) * 128], bk[0:1, 0:128], r=[bb_], w=bXS)
            bk, bb_ = nb()
            mm(bk[:, 0:128], ROW_LNVB[0:1, g * 128:(g + 1) * 128], W1Rv[0:1, g * 128:(g + 1) * 128], True, False,
               r=bXS, w=[bb_])
            mm(bk[:, 0:128], ROW_ONE, ROW_BSP[0:1, g * 128:(g + 1) * 128], False, True, r=bXS, w=[bb_])
            cp("dve", BB[:, g, :], bk[:, 0:128], r=[bb_], w=[bWT])

        NCHUNK = BL * NCH

        def ln_stats_a(XN, bXN):
            for t4 in range(4):
                for h in range(2):
                    S_.op("dve", lambda e, t4=t4, h=h: e.bn_stats(out=ST6[:, t4, h, :], in_=XN[:, t4, h * 512:(h + 1) * 512]),
                          [bXN[t4]], [bSTAT])
                S_.op("dve", lambda e, t4=t4: e.bn_aggr(out=MV[:, t4, :], in_=ST6[:, t4, :, :]), [bSTAT], [bSTAT])
            ts("dve", VE[:, :], MV[:, :, 1], LN_EPS, None, ALU.add, r=[bSTAT], w=[bSTAT])
            tt("pool", RS[:, :], VE[:, :], MHALF[:, :], ALU.pow, r=[bSTAT, bC], w=[bSTAT])
            stt(NMR[:, :], MV[:, :, 0], -1.0, RS[:, :], ALU.mult, ALU.mult, r=[bSTAT], w=[bSTAT])

        def ln_norm(XN, bXN):
            for t4 in range(4):
                act(XN[:, t4, :], XN[:, t4, :], AF.Identity, bias=NMR[:, t4:t4 + 1], scale=RS[:, t4:t4 + 1],
                    r=[bXN[t4], bSTAT], w=[bXN[t4]])

        def xload(ch):
            XN, bXN = XNs[ch % 2], bXNs[ch % 2]
            tok0 = ch * T
            ld("pool", XN[:, :, :], x[tok0:tok0 + T, :].rearrange("(t p) d -> p t d", p=128), "xin%d" % (ch % 2),
               r=[], w=bXN)

        def inproj_chunk(wv, wb, j, f):
            cc = j * 4 + f
            bk, bb_ = nb()
            for kc in range(KC):
                mm(bk[:, :], wv[:, kc, f * 128:(f + 1) * 128], hT[:, kc, :], kc == 0, kc == KC - 1,
                   r=[wb, bhT[kc]], w=[bb_])
            bias = BINC[:, cc:cc + 1]
            if cc < 10:
                act(XL[:, cc, 4:4 + T], bk[:, :], AF.Identity, bias=bias, r=[bb_, bC], w=[bXL[cc]])
            elif cc < 20:
                act(GG[:, cc - 10, :], bk[:, :], AF.Gelu_apprx_tanh, bias=bias, r=[bb_, bC], w=[bGG[cc - 10]])
            elif cc < 26:
                act(UG[:, cc - 20, :], bk[:, :], AF.Gelu_apprx_tanh, bias=bias, r=[bb_, bC], w=[bUG[cc - 20]])
            else:
                act(SIG[:, cc - 32, :], bk[:, :], AF.Tanh, bias=BINH[:, cc:cc + 1], scale=0.5,
                    r=[bb_, bC], w=[bSIG[cc - 32]])

        def Ma_t(ch):
            XN, bXN = XNs[ch % 2], bXNs[ch % 2]
            b = ch // NCH
            for kc in range(KC):
                bk, bb_ = nb()
                for t4 in range(4):
                    tr(bk[:, t4 * 128:(t4 + 1) * 128], XN[:, t4, kc * 128:(kc + 1) * 128], r=[bXN[t4]], w=[bb_])
                act(hT[:, kc, :], bk[:, :], AF.Identity, bias=MOD[:, kc, b:b + 1], scale=S1P[:, kc, b:b + 1],
                    r=[bb_, bMOD, bDER], w=[bhT[kc]])

        def Ma_p(ch, pieces=(0, 1, 2)):
            ci = ch % NCH
            if ci == 0 and 0 in pieces:
                S_.op("dve", lambda e: e.memset(XL[:, :, 0:4], 0.0), [], bXL)
            for j in pieces:
                wv, wb = load_piece("w_in", j)
                for f in range(4):
                    inproj_chunk(wv, wb, j, f)

        def conv(c):
            ta, tab = rF.next()
            ts("dve", ta[:, :], XL[:, c, 4:4 + T], CVW[:, c, 3:4], CVB[:, c:c + 1], ALU.mult, ALU.add,
               r=[bXL[c], bC], w=[tab])
            stt(ta[:, :], XL[:, c, 3:3 + T], CVW[:, c, 2:3], ta[:, :], ALU.mult, ALU.add, r=[bXL[c], tab, bC], w=[tab])
            stt(ta[:, :], XL[:, c, 2:2 + T], CVW[:, c, 1:2], ta[:, :], ALU.mult, ALU.add, r=[bXL[c], tab, bC], w=[tab])
            stt(BIG[:, c, :], XL[:, c, 1:1 + T], CVW[:, c, 0:1], ta[:, :], ALU.mult, ALU.add,
                r=[bXL[c], tab, bC], w=[bBIG[c]])
            cp("dve", XL[:, c, 0:4], XL[:, c, T:T + 4], r=[bXL[c]], w=[bXL[c]])

        def lru_gates(gv, gb, c):
            bkr, bbr = nb()
            mm(bkr[:, :], gv[:, c, :], BIG[:, c, :], True, True, r=[gb, bBIG[c]], w=[bbr])
            bki, bbi = nb()
            mm(bki[:, :], gv[:, 10 + c, :], BIG[:, c, :], True, True, r=[gb, bBIG[c]], w=[bbi])
            act(AR[:, c, :], bkr[:, :], AF.Tanh, bias=BRAH[:, c:c + 1], scale=0.5, r=[bbr, bC], w=[bAR[c]])
            act(BIG[:, 10 + c, :], bki[:, :], AF.Tanh, bias=BRXH[:, c:c + 1], scale=0.5, r=[bbi, bC], w=[bBIG[10 + c]])
            act(AR[:, c, :], AR[:, c, :], AF.Exp, bias=CCH[:, c:c + 1], scale=CCH[:, c:c + 1], r=[bAR[c], bC], w=[bAR[c]])
            tq, tqb = rF.next()
            stt(tq[:, :], AR[:, c, :], -1.0, AR[:, c, :], ALU.mult, ALU.mult, r=[bAR[c]], w=[tqb])
            ts("dve", BIG[:, 20 + c, :], tq[:, :], 1.0, None, ALU.add, r=[tqb], w=[bBIG[20 + c]])

        def sgu(ch):
            wv6, wb6 = load_piece("w_in", 6)
            wv, wb = load_piece("w_in", 7)
            ones_b = ROWB[0:1, SW:SW + 128]
            for t4 in range(4):
                b6, bb6 = nb()
                for kc in range(KC):
                    mm(b6[:, 0:256], hT[:, kc, t4 * 128:(t4 + 1) * 128], wv6[:, kc, 256:512], kc == 0, False,
                       r=[wb6, bhT[kc]], w=[bb6])
                mm(b6[:, 0:256], ones_b, ROWB[0:1, 0:256], False, True, r=[bROW], w=[bb6])
                b7, bb7 = nb()
                for kc in range(KC):
                    mm(b7[:, 0:512], hT[:, kc, t4 * 128:(t4 + 1) * 128], wv[:, kc, 0:512], kc == 0, False,
                       r=[wb, bhT[kc]], w=[bb7])
                mm(b7[:, 0:512], ones_b, ROWB[0:1, 256:768], False, True, r=[bROW], w=[bb7])
                vt, vb_ = rV.next()
                act(vt[:, 0:256], b6[:, 0:256], AF.Gelu_apprx_tanh, r=[bb6], w=[vb_])
                act(vt[:, 256:768], b7[:, 0:512], AF.Gelu_apprx_tanh, r=[bb7], w=[vb_])
                S_.op("dve", lambda e, vt=vt: e.bn_stats(out=ST6V[:, 0, :], in_=vt[:, 0:384]), [vb_], [bSTATV])
                S_.op("dve", lambda e, vt=vt: e.bn_stats(out=ST6V[:, 1, :], in_=vt[:, 384:768]), [vb_], [bSTATV])
                S_.op("dve", lambda e: e.bn_aggr(out=MVV[:, :], in_=ST6V[:, :, :]), [bSTATV], [bSTATV])
                ts("dve", VEV[:, 0:1], MVV[:, 1:2], LN_EPS, None, ALU.add, r=[bSTATV], w=[bSTATV])
                tt("pool", RSV[:, 0:1], VEV[:, 0:1], MHALF[:, 0:1], ALU.pow, r=[bSTATV, bC], w=[bSTATV])
                stt(NMRV[:, 0:1], MVV[:, 0:1], -1.0, RSV[:, 0:1], ALU.mult, ALU.mult, r=[bSTATV], w=[bSTATV])
                act(VN[:, t4, :], vt[:, :], AF.Identity, bias=NMRV[:, 0:1], scale=RSV[:, 0:1],
                    r=[vb_, bSTATV], w=[bVN[t4]])
            wv5, wb5 = load_piece("w_in", 5)
            for f in range(4):
                inproj_chunk(wv5, wb5, 5, f)
            for f in range(2):
                inproj_chunk(wv6, wb6, 6, f)
            for g in range(SC):
                bk, bb_ = nb()
                for t4 in range(4):
                    mm(bk[:, t4 * 128:(t4 + 1) * 128], VN[:, t4, g * 128:(g + 1) * 128], WTB[:, g, :], True, True,
                       r=[bVN[t4], bWT], w=[bb_])
                tf, tfb = rF.next()
                bbv = bass.AP(BB[:, :, :].tensor, g * 128, [[SC * 128, 128], [0, 4], [1, 128]])
                stt(v3(tf[:, :], 4), v3(bk[:, :], 4), LVG[:, g:g + 1], bbv, ALU.mult, ALU.add,
                    r=[bb_, bC, bWT], w=[tfb])
                tt("dve", UG[:, g, :], tf[:, :], UG[:, g, :], ALU.mult, r=[tfb, bUG[g]], w=[bUG[g]])


        def Mb(ch, hook1=None, hook2=None, hook3=None):
            XN, bXN = XNs[ch % 2], bXNs[ch % 2]
            b = ch // NCH
            ci = ch % NCH
            if hook1 is not None:
                hook1()
            for c in range(LC):
                conv(c)
                if c == 4 and hook2 is not None:
                    hook2()
            for j in (3, 4):
                wv, wb = load_piece("w_in", j)
                for f in range(4):
                    inproj_chunk(wv, wb, j, f)
                mod_step()
            gv, gb = load_piece("rg", 0)
            k_rg = last_slot[0]
            n = 0
            for j in (8, 9, 10, 11):
                wv, wb = load_piece("w_in", j, avoid=k_rg)
                for f in range(4):
                    inproj_chunk(wv, wb, j, f)
                    if 6 <= n < 6 + LC:
                        lru_gates(gv, gb, n - 6)
                    n += 1
                mod_step(avoid=k_rg)
            for c in range(LC):
                act(BIG[:, 20 + c, :], BIG[:, 20 + c, :], AF.Sqrt, scale=0.25, r=[bBIG[20 + c]], w=[bBIG[20 + c]])
            for c in range(LC):
                stt(BIG[:, 20 + c, :], BIG[:, 10 + c, :], 1.0, BIG[:, 20 + c, :], ALU.add, ALU.mult,
                    r=[bBIG[20 + c], bBIG[10 + c]], w=[bBIG[20 + c]])
            for c in range(LC):
                tt("dve", BIG[:, c, :], BIG[:, 20 + c, :], BIG[:, c, :], ALU.mult,
                   r=[bBIG[20 + c], bBIG[c]], w=[bBIG[c]])
            for c in range(LC):
                th, thb = rF.next()
                init = 0.0 if ci == 0 else HST[:, c:c + 1]
                S_.op("dve", lambda e, th=th, c=c, init=init: e.tensor_tensor_scan(
                    out=th[:, :], data0=AR[:, c, :], data1=BIG[:, c, :], initial=init, op0=ALU.mult, op1=ALU.add),
                    [bAR[c], bBIG[c], bHST[c]], [thb])
                cp("dve", HST[:, c:c + 1], th[:, T - 1:T], r=[thb], w=[bHST[c]])
                tt("dve", BIG[:, 10 + c, :], th[:, :], GG[:, c, :], ALU.mult, r=[thb, bGG[c]], w=[bBIG[10 + c]])
            ybank = []
            psg = [load_piece("w_o_sgu", 0, avoid=k_rg), load_piece("w_o_sgu", 1)]
            t2s = []
            for oc in range(KC):
                jj, f = oc // 4, oc % 4
                wv2, wb2 = psg[jj]
                bkb, bbb = nb()
                for g in range(SC):
                    mm(bkb[:, :], wv2[:, g, f * 128:(f + 1) * 128], UG[:, g, :], g == 0, g == SC - 1,
                       r=[wb2, bUG[g]], w=[bbb])
                act(MG[:, oc, :], bkb[:, :], AF.Copy, r=[bbb], w=[bMG[oc]])
            if hook3 is not None:
                hook3()
            for oc in range(KC):
                stt(MG[:, oc, :], SIG[:, 8 + oc, :], 1.0, MG[:, oc, :], ALU.add, ALU.mult,
                    r=[bSIG[8 + oc], bMG[oc]], w=[bMG[oc]])
            pa = {}
            for oc in range(KC):
                jj, f = oc // 2, oc % 2
                if jj not in pa:
                    pa[jj] = load_piece("w_o_lru", jj)
                wv, wb = pa[jj]
                bka, bba = nb()
                for c in range(LC):
                    mm(bka[:, :], wv[:, c, f * 128:(f + 1) * 128], BIG[:, 10 + c, :], c == 0, c == LC - 1,
                       r=[wb, bBIG[10 + c]], w=[bba])
                t1, t1b = rF.next()
                stt(t1[:, :], SIG[:, oc, :], 1.0, bka[:, :], ALU.add, ALU.mult, r=[bba, bSIG[oc]], w=[t1b])
                tt("dve", MG[:, oc, :], t1[:, :], MG[:, oc, :], ALU.add, r=[t1b, bMG[oc]], w=[bMG[oc]])

            prev = None
            for oc in range(KC + 1):
                cur = None
                if oc < KC:
                    jj, f = oc // 4, oc % 4
                    if f == 0:
                        wv, wb = load_piece("w_out", jj)
                    bk, bb_ = nb()
                    for kc in range(KC):
                        mm(bk[:, :], wv[:, kc, f * 128:(f + 1) * 128], MG[:, kc, :], kc == 0, kc == KC - 1,
                           r=[wb, bMG[kc]], w=[bb_])
                    tm, tmb = rF.next()
                    act(tm[:, :], bk[:, :], AF.Identity, scale=G1P[:, oc, b:b + 1], r=[bb_, bDER], w=[tmb])
                    cur = (oc, tm, tmb)
                if prev is not None:
                    po, tm_, tmb_ = prev
                    bk2, bb2 = nb()
                    for t4 in range(4):
                        tr(bk2[:, t4 * 128:(t4 + 1) * 128], tm_[:, t4 * 128:(t4 + 1) * 128], r=[tmb_], w=[bb2])
                    stt(XN[:, :, po * 128:(po + 1) * 128], XN[:, :, po * 128:(po + 1) * 128], ALPHA, v3(bk2[:, :], 4),
                        ALU.mult, ALU.add, r=[bb2] + bXN, w=bXN)
                prev = cur
            ln_stats_a(XN, bXN)

        def Fa(ch):
            XN, bXN = XNs[ch % 2], bXNs[ch % 2]
            b = ch // NCH
            for kc in range(KC):
                bk, bb_ = nb()
                for t4 in range(4):
                    tr(bk[:, t4 * 128:(t4 + 1) * 128], XN[:, t4, kc * 128:(kc + 1) * 128], r=[bXN[t4]], w=[bb_])
                act(H2[:, kc, :], bk[:, :], AF.Identity, bias=B2[:, kc, b:b + 1], scale=A2[:, kc, b:b + 1],
                    r=[bb_, bDER], w=[bH2[kc]])
                act(AR[:, kc, :], bk[:, :], AF.Identity, bias=AB1[:, kc:kc + 1], scale=AG1[:, kc:kc + 1],
                    r=[bb_, bC], w=[bAR[kc]])

        def Fb(ch):
            XN, bXN = XNs[ch % 2], bXNs[ch % 2]
            b = ch // NCH
            tok0 = ch * T
            for j in range(8):
                wv, wb = load_piece("w_up", j)
                for f in range(4):
                    fc = j * 4 + f
                    bk, bb_ = nb()
                    for kc in range(KC):
                        mm(bk[:, :], wv[:, kc, f * 128:(f + 1) * 128], H2[:, kc, :], kc == 0, kc == KC - 1,
                           r=[wb, bH2[kc]], w=[bb_])
                    tr_, trb = rF.next()
                    act(tr_[:, :], bk[:, :], AF.Relu, r=[bb_], w=[trb])
                    tt("dve", BIG[:, fc, :], tr_[:, :], tr_[:, :], ALU.mult, r=[trb], w=[bBIG[fc]])
                mod_step()
            prev = None
            for oc in range(KC + 1):
                cur = None
                if oc < KC:
                    wv, wb = load_piece("w_down", oc)
                    bk, bb_ = nb()
                    for fc in range(FC):
                        mm(bk[:, :], wv[:, fc, :], BIG[:, fc, :], fc == 0, fc == FC - 1, r=[wb, bBIG[fc]], w=[bb_])
                    stt(AR[:, oc, :], bk[:, :], G2P[:, oc, b:b + 1], AR[:, oc, :], ALU.mult, ALU.add,
                        r=[bb_, bDER, bAR[oc]], w=[bAR[oc]])
                    cur = oc
                if prev is not None:
                    bk2, bb2 = nb()
                    for t4 in range(4):
                        tr(bk2[:, t4 * 128:(t4 + 1) * 128], AR[:, prev, t4 * 128:(t4 + 1) * 128], r=[bAR[prev]], w=[bb2])
                    act(XN[:, :, prev * 128:(prev + 1) * 128], v3(bk2[:, :], 4), AF.Copy, r=[bb2], w=bXN)
                prev = cur

        def Fb_tail_a(ch):
            ln_stats_a(XNs[ch % 2], bXNs[ch % 2])

        def Fb_tail_b(ch):
            XN, bXN = XNs[ch % 2], bXNs[ch % 2]
            tok0 = ch * T
            ln_norm(XN, bXN)
            for t4 in range(4):
                tt("dve", XN[:, t4, :], XN[:, t4, :], G2B[:, :], ALU.mult, r=[bXN[t4], bG2B], w=[bXN[t4]])
                tt("dve", XN[:, t4, :], XN[:, t4, :], B2B[:, :], ALU.add, r=[bXN[t4], bG2B], w=[bXN[t4]])
            ld("pool", out[tok0:tok0 + T, :].rearrange("(t p) d -> p t d", p=128), XN[:, :, :], "xout%d" % (ch % 2),
               r=bXN, w=[])

        xload(0)
        xload(1)
        Ma_t(0)
        Ma_p(0)
        sgu(0)
        for ch in range(NCHUNK):
            def hook1(ch=ch):
                if ch >= 1:
                    Fb_tail_a(ch - 1)

            def hook2(ch=ch):
                if ch >= 1:
                    Fb_tail_b(ch - 1)
                    if ch + 1 < NCHUNK:
                        xload(ch + 1)

            def hook3(ch=ch):
                if ch + 1 < NCHUNK:
                    Ma_t(ch + 1)
                    Ma_p(ch + 1)
            Mb(ch, hook1, hook2, hook3)
            if ch + 1 < NCHUNK:
                sgu(ch + 1)
            ln_norm(XNs[ch % 2], bXNs[ch % 2])
            Fa(ch)
            Fb(ch)
        Fb_tail_a(NCHUNK - 1)
        Fb_tail_b(NCHUNK - 1)

        S_.op("pool", lambda e: e.memset(JUNK[:, :], 0.0), [], bXNs[0] + bXNs[1] + [bJ])
        S_.emit(nc)
    return nc


_NC_CACHE = {}


def _col(v, n):
    return np.ascontiguousarray(np.asarray(v, np.float32).reshape(n, 128).T)


def kernel(x, c, w_ada, b_ada, w_in, b_in, w_conv, b_conv, w_rg_a, b_rg_a, w_rg_x, b_rg_x,
           lru_lambda, w_sp, b_sp, ln_v_g, ln_v_b, w_o_lru, w_o_sgu, w_out, ln1_g, ln1_b,
           w_up, w_down, ln2_g, ln2_b):
    f = lambda a: np.ascontiguousarray(np.asarray(a, dtype=np.float32))
    x = f(x)
    c = f(c)
    if "nc" not in _NC_CACHE:
        _NC_CACHE["nc"] = build_nc()
    nc = _NC_CACHE["nc"]
    shared = {
        "w_ada": f(w_ada), "b_ada_c": _col(b_ada[0], 48),
        "w_in": f(w_in), "b_in_c": _col(b_in[0], 48),
        "b_in_v": f(np.asarray(b_in)[0:1, 3328:4096]),
        "convw_c": np.ascontiguousarray(np.asarray(w_conv, np.float32)[0].reshape(4, LC, 128).transpose(2, 1, 0)),
        "convb_c": _col(b_conv[0], LC),
        "w_rg_a": f(w_rg_a), "w_rg_x": f(w_rg_x),
        "b_rga_c": _col(b_rg_a[0], LC), "b_rgx_c": _col(b_rg_x[0], LC), "lam_c": _col(lru_lambda[0], LC),
        "w_sp": f(w_sp), "bsp_row": f(np.asarray(b_sp)[0].reshape(1, SW)),
        "lnvg_c": _col(ln_v_g[0], SC), "lnvb_row": f(np.asarray(ln_v_b)[0:1]),
        "w_o_lru": f(w_o_lru), "w_o_sgu": f(w_o_sgu), "w_out": f(w_out), "w_up": f(w_up), "w_down": f(w_down),
        "ln1g_c": _col(ln1_g[0], KC), "ln1b_c": _col(ln1_b[0], KC),
        "g2b": np.ascontiguousarray(np.broadcast_to(np.asarray(ln2_g, np.float32)[0:1], (128, D))),
        "b2b": np.ascontiguousarray(np.broadcast_to(np.asarray(ln2_b, np.float32)[0:1], (128, D))),
        "ident": np.eye(128, dtype=np.float32),
    }
    in_maps = []
    for i in range(NCORES):
        m = dict(shared)
        m["x"] = x[i * BL:(i + 1) * BL].reshape(BL * S, D)
        ci = c[i * BL:(i + 1) * BL]
        m["c_t"] = np.ascontiguousarray(ci.reshape(BL, KC, 128).transpose(2, 1, 0))
        in_maps.append(m)
    res = run_bass_kernel_spmd(nc, in_maps, core_ids=list(range(NCORES)))
    outs = [np.asarray(r["out"], dtype=np.float32).reshape(BL, S, D) for r in res.results]
    return np.concatenate(outs, axis=0)
```
